# Optimizing a Trainium2 kernel written in Bass

```python
import jax, jax.numpy as jnp
from jax import lax
import numpy as np


D_MODEL = 2048
BATCH = 1
SEQ = 8192
DEPTH = 4

GRID_W = 64
CTX_LEN = 256
N_MIXERS = 3
Q_BLOCK = 128
EPS = 1e-6
ROPE_THETA = 10000.0
D_FF = 4 * D_MODEL

A_HEADS = 16
A_KV_HEADS = 4
A_HEAD_DIM = D_MODEL // A_HEADS
A_GROUP = A_HEADS // A_KV_HEADS

CONV_WIDTH = 3

C_HEADS = 16
C_NOPE = 128
C_ROPE = 64
C_V = 128
C_Q_RANK = 512
C_KV_RANK = 512

N_A = (DEPTH - 0 + N_MIXERS - 1) // N_MIXERS
N_B = (DEPTH - 1 + N_MIXERS - 1) // N_MIXERS
N_C = (DEPTH - 2 + N_MIXERS - 1) // N_MIXERS

kernel_name = 'hybrid_dit_gqa_shortconv_mla'


def rms_norm(x, g):
    xf = x.astype(jnp.float32)
    y = xf * lax.rsqrt(jnp.mean(xf * xf, axis=-1, keepdims=True) + EPS)
    return (y * g.astype(jnp.float32)).astype(x.dtype)


def modulate(h, shift, scale):
    return h * (1.0 + scale) + shift


def axial_rope_tables(row, col, rot_dim):
    n_axis = rot_dim // 4
    inv = ROPE_THETA ** (-jnp.arange(n_axis, dtype=jnp.float32) / n_axis)
    ang = jnp.concatenate([row.astype(jnp.float32)[:, None] * inv,
                           col.astype(jnp.float32)[:, None] * inv], axis=-1)
    return jnp.cos(ang), jnp.sin(ang)


def apply_rope(x, cos, sin):
    half = x.shape[-1] // 2
    x1, x2 = x[..., :half], x[..., half:]
    cs = cos[None, :, None, :].astype(x.dtype)
    sn = sin[None, :, None, :].astype(x.dtype)
    return jnp.concatenate([x1 * cs - x2 * sn, x1 * sn + x2 * cs], axis=-1)


def blocked_attention(q, k, v, scale):
    b, s, kh, g, dk = q.shape
    dv = v.shape[-1]
    nb = s // Q_BLOCK
    qb = q.reshape(b, nb, Q_BLOCK, kh, g, dk).transpose(1, 0, 2, 3, 4, 5)

    def one_block(qblk):
        sc = jnp.einsum('bqhgd,bkhd->bhgqk', qblk, k).astype(jnp.float32) * scale
        p = jax.nn.softmax(sc, axis=-1).astype(v.dtype)
        return jnp.einsum('bhgqk,bkhd->bqhgd', p, v)

    o = lax.map(one_block, qb)
    return o.transpose(1, 0, 2, 3, 4, 5).reshape(b, s, kh, g, dv)


def gqa_mixer(h, hc, w_qkv, q_gain, k_gain, w_o, cos, sin, ctx_out):
    def proj(z):
        b, l, _ = z.shape
        qkv = z @ w_qkv
        q = qkv[..., :A_HEADS * A_HEAD_DIM].reshape(b, l, A_HEADS, A_HEAD_DIM)
        k = qkv[..., A_HEADS * A_HEAD_DIM:(A_HEADS + A_KV_HEADS) * A_HEAD_DIM].reshape(b, l, A_KV_HEADS, A_HEAD_DIM)
        v = qkv[..., (A_HEADS + A_KV_HEADS) * A_HEAD_DIM:].reshape(b, l, A_KV_HEADS, A_HEAD_DIM)
        return rms_norm(q, q_gain), rms_norm(k, k_gain), v

    q, k, v = proj(h)
    qc, kc, vc = proj(hc)
    q = apply_rope(q, cos, sin)
    k = apply_rope(k, cos, sin)
    b, s = h.shape[0], h.shape[1]
    scale = A_HEAD_DIM ** -0.5
    k_all = jnp.concatenate([k, kc], axis=1)
    v_all = jnp.concatenate([v, vc], axis=1)
    o = blocked_attention(q.reshape(b, s, A_KV_HEADS, A_GROUP, A_HEAD_DIM), k_all, v_all, scale)
    y = o.reshape(b, s, A_HEADS * A_HEAD_DIM) @ w_o
    yc = None
    if ctx_out:
        lc = hc.shape[1]
        oc = blocked_attention(qc.reshape(b, lc, A_KV_HEADS, A_GROUP, A_HEAD_DIM), kc, vc, scale)
        yc = oc.reshape(b, lc, A_HEADS * A_HEAD_DIM) @ w_o
    return y, yc


def depthwise_conv_centred(z, w):
    pad = (CONV_WIDTH - 1) // 2
    return lax.conv_general_dilated(
        z, w[:, None, :].astype(z.dtype), window_strides=(1,), padding=[(pad, pad)],
        dimension_numbers=('NWC', 'WIO', 'NWC'), feature_group_count=z.shape[-1])


def short_conv_mixer(h, hc, w_in, conv_w, w_out, ctx_out):
    def run(z):
        bg, cg, u = jnp.split(z @ w_in, 3, axis=-1)
        return (bg * depthwise_conv_centred(cg * u, conv_w)) @ w_out
    y = run(h)
    yc = run(hc) if ctx_out else None
    return y, yc


def mla_mixer(h, hc, w_dq, q_gain, w_uq, w_dkv, kv_gain, w_ukv, w_o, cos, sin, ctx_out):
    def proj(z):
        b, l, _ = z.shape
        q = (rms_norm(z @ w_dq, q_gain) @ w_uq).reshape(b, l, C_HEADS, C_NOPE + C_ROPE)
        ckv_kr = z @ w_dkv
        ckv = rms_norm(ckv_kr[..., :C_KV_RANK], kv_gain)
        k_rope = ckv_kr[..., C_KV_RANK:][:, :, None, :]
        kv = (ckv @ w_ukv).reshape(b, l, C_HEADS, C_NOPE + C_V)
        return q[..., :C_NOPE], q[..., C_NOPE:], kv[..., :C_NOPE], k_rope, kv[..., C_NOPE:]

    def assemble(q_nope, q_rope, k_nope, k_rope, v):
        b, l = q_nope.shape[0], q_nope.shape[1]
        qf = jnp.concatenate([q_nope, q_rope], axis=-1)
        kf = jnp.concatenate([k_nope, jnp.broadcast_to(k_rope, (b, l, C_HEADS, C_ROPE))], axis=-1)
        return qf, kf, v

    qn, qr, kn, kr, v = proj(h)
    qr = apply_rope(qr, cos, sin)
    kr = apply_rope(kr, cos, sin)
    q, k, v = assemble(qn, qr, kn, kr, v)
    qc, kc, vc = assemble(*proj(hc))
    b, s = h.shape[0], h.shape[1]
    scale = (C_NOPE + C_ROPE) ** -0.5
    k_all = jnp.concatenate([k, kc], axis=1)
    v_all = jnp.concatenate([v, vc], axis=1)
    o = blocked_attention(q[:, :, :, None, :], k_all, v_all, scale)
    y = o.reshape(b, s, C_HEADS * C_V) @ w_o
    yc = None
    if ctx_out:
        lc = hc.shape[1]
        oc = blocked_attention(qc[:, :, :, None, :], kc, vc, scale)
        yc = oc.reshape(b, lc, C_HEADS * C_V) @ w_o
    return y, yc


def sq_relu_mlp(h, w1, w2):
    return jnp.square(jax.nn.relu(h @ w1)) @ w2


def setup_inputs(seed: int = 0) -> dict:
    key = jax.random.key(seed)
    ks = jax.random.split(key, 32)
    f32 = jnp.float32

    def nrm(k, shape, fan_in, gain=1.0):
        return jax.random.normal(k, shape, f32) * (gain * fan_in ** -0.5)

    def gain_(k, shape):
        return 1.0 + 0.02 * jax.random.normal(k, shape, f32)

    d = D_MODEL
    qkv_w = (A_HEADS + 2 * A_KV_HEADS) * A_HEAD_DIM
    return {
        'x': jax.random.normal(ks[0], (BATCH, SEQ, d), f32),
        'c': jax.random.normal(ks[1], (BATCH, d), f32),
        'ctx': jax.random.normal(ks[2], (BATCH, CTX_LEN, d), f32),
        'c_ctx': jax.random.normal(ks[3], (d,), f32),
        'w_ada': nrm(ks[4], (DEPTH, d, 6 * d), d, 0.5),
        'b_ada': 0.02 * jax.random.normal(ks[5], (DEPTH, 6 * d), f32),
        'norm1': gain_(ks[6], (DEPTH, d)),
        'norm2': gain_(ks[7], (DEPTH, d)),
        'w_mlp1': nrm(ks[8], (DEPTH, d, D_FF), d),
        'w_mlp2': nrm(ks[9], (DEPTH, D_FF, d), D_FF),
        'a_w_qkv': nrm(ks[10], (N_A, d, qkv_w), d),
        'a_q_norm': gain_(ks[11], (N_A, A_HEAD_DIM)),
        'a_k_norm': gain_(ks[12], (N_A, A_HEAD_DIM)),
        'a_w_o': nrm(ks[13], (N_A, A_HEADS * A_HEAD_DIM, d), A_HEADS * A_HEAD_DIM),
        'b_w_in': nrm(ks[14], (N_B, d, 3 * d), d),
        'b_conv': nrm(ks[15], (N_B, CONV_WIDTH, d), CONV_WIDTH),
        'b_w_out': nrm(ks[16], (N_B, d, d), d),
        'c_w_dq': nrm(ks[17], (N_C, d, C_Q_RANK), d),
        'c_q_norm': gain_(ks[18], (N_C, C_Q_RANK)),
        'c_w_uq': nrm(ks[19], (N_C, C_Q_RANK, C_HEADS * (C_NOPE + C_ROPE)), C_Q_RANK),
        'c_w_dkv': nrm(ks[20], (N_C, d, C_KV_RANK + C_ROPE), d),
        'c_kv_norm': gain_(ks[21], (N_C, C_KV_RANK)),
        'c_w_ukv': nrm(ks[22], (N_C, C_KV_RANK, C_HEADS * (C_NOPE + C_V)), C_KV_RANK),
        'c_w_o': nrm(ks[23], (N_C, C_HEADS * C_V, d), C_HEADS * C_V),
        'final_norm': gain_(ks[24], (d,)),
    }


def reference(x, c, ctx, c_ctx, w_ada, b_ada, norm1, norm2, w_mlp1, w_mlp2,
              a_w_qkv, a_q_norm, a_k_norm, a_w_o, b_w_in, b_conv, b_w_out,
              c_w_dq, c_q_norm, c_w_uq, c_w_dkv, c_kv_norm, c_w_ukv, c_w_o, final_norm):
    seq = x.shape[1]
    rows = seq // GRID_W
    row = jnp.repeat(jnp.arange(rows, dtype=jnp.int32), GRID_W)
    col = jnp.tile(jnp.arange(GRID_W, dtype=jnp.int32), rows)
    cos_a, sin_a = axial_rope_tables(row, col, A_HEAD_DIM)
    cos_c, sin_c = axial_rope_tables(row, col, C_ROPE)

    silu_c = jax.nn.silu(c)
    silu_cc = jax.nn.silu(c_ctx)
    xc = ctx
    for i in range(DEPTH):
        ctx_out = i < DEPTH - 1
        mixer, j = i % N_MIXERS, i // N_MIXERS
        sh1, sc1, g1, sh2, sc2, g2 = jnp.split((silu_c @ w_ada[i] + b_ada[i])[:, None, :], 6, axis=-1)
        csh1, csc1, cg1, csh2, csc2, cg2 = jnp.split(silu_cc @ w_ada[i] + b_ada[i], 6, axis=-1)
        h = modulate(rms_norm(x, norm1[i]), sh1, sc1)
        hc = modulate(rms_norm(xc, norm1[i]), csh1, csc1)
        if mixer == 0:
            y, yc = gqa_mixer(h, hc, a_w_qkv[j], a_q_norm[j], a_k_norm[j], a_w_o[j], cos_a, sin_a, ctx_out)
        elif mixer == 1:
            y, yc = short_conv_mixer(h, hc, b_w_in[j], b_conv[j], b_w_out[j], ctx_out)
        else:
            y, yc = mla_mixer(h, hc, c_w_dq[j], c_q_norm[j], c_w_uq[j], c_w_dkv[j], c_kv_norm[j],
                              c_w_ukv[j], c_w_o[j], cos_c, sin_c, ctx_out)
        x = x + g1 * y
        x = x + g2 * sq_relu_mlp(modulate(rms_norm(x, norm2[i]), sh2, sc2), w_mlp1[i], w_mlp2[i])
        if ctx_out:
            xc = xc + cg1 * yc
            xc = xc + cg2 * sq_relu_mlp(modulate(rms_norm(xc, norm2[i]), csh2, csc2), w_mlp1[i], w_mlp2[i])
    return rms_norm(x, final_norm)
```

```python
import math
from contextlib import ExitStack
import numpy as np
import concourse.bass as bass
import concourse.mybir as mybir
from concourse.bass_utils import run_bass_kernel_spmd

F32 = mybir.dt.float32
BF16 = mybir.dt.bfloat16
ALU = mybir.AluOpType
AF = mybir.ActivationFunctionType

FUSED = True
SAME_ENGINE_SYNC = True
NCORES = 8
D = 2048
KC = 16
NL = 1024
NCX = 256
NT = NL + NCX
TILES = [(0, 512), (512, 512), (1024, 256)]
EPS = 1e-6
NKEY = 8192 + 256
NCH = NKEY // 128
WSLOT = 4224
NSLOT = 4


class _Op:
    __slots__ = ("eng", "fn", "reads", "writes", "slot", "deps", "sig", "cnt", "idx", "ep")

    def __init__(self, eng, fn, reads, writes, slot):
        self.eng, self.fn, self.reads, self.writes, self.slot = eng, fn, reads, writes, slot
        self.deps = ()
        self.sig = False
        self.cnt = 0


class _Rec:
    def __init__(self):
        self.calls = []

    def __getattr__(self, name):
        def f(*args, **kwargs):
            self.calls.append((name, args, kwargs))
        return f


class Prog:
    ENGS = ("pe", "act", "dve", "pool", "sp")

    def __init__(self, nc):
        self.nc = nc
        self.ops = []
        self.stack = ExitStack()
        self.epoch = 0

    def sb(self, name, shape, dtype):
        return self.stack.enter_context(self.nc.sbuf_tensor("sb_" + name, list(shape), dtype))

    def ps(self, name, shape, dtype=F32):
        return self.stack.enter_context(self.nc.psum_tensor(name, list(shape), dtype))

    def op(self, eng, fn, reads=(), writes=(), slot=None):
        rec = _Rec()
        fn(rec)
        assert len(rec.calls) == 1
        o = _Op(eng, rec.calls[0], tuple(reads), tuple(writes), slot)
        o.idx = len(self.ops)
        o.ep = self.epoch
        self.ops.append(o)
        return o

    def dma(self, eng, out, in_, slot, reads=(), writes=(), **kw):
        return self.op(eng, lambda e: e.dma_start(out=out, in_=in_, **kw), reads, writes, slot)

    def build(self):
        nc, ops = self.nc, self.ops
        lw, rs = {}, {}
        slot_last = {}
        for o in ops:
            deps = set()
            if o.slot is not None:
                key = o.writes if o.writes else o.reads
                prev = slot_last.get(o.slot)
                if prev is not None and prev[0] != key:
                    deps.add(prev[1])
                slot_last[o.slot] = (key, o.idx)
            for r in o.reads:
                deps.update(lw.get(r, ()))
            for w in o.writes:
                deps.update(lw.get(w, ()))
                deps.update(rs.get(w, ()))
            for r in o.reads:
                rs.setdefault(r, []).append(o.idx)
            for w in o.writes:
                prev = lw.get(w, [])
                if o.slot is not None and prev and all(ops[j].slot is not None for j in prev):
                    keep = {}
                    for j in prev + [o.idx]:
                        keep[ops[j].slot] = j
                    lw[w] = sorted(keep.values())
                else:
                    lw[w] = [o.idx]
                rs[w] = []
            deps.discard(o.idx)
            o.deps = sorted(deps)
        for o in ops:
            for j in o.deps:
                d = ops[j]
                if d.slot is not None:
                    continue
                if d.eng != o.eng or o.slot is not None:
                    d.sig = True
                elif SAME_ENGINE_SYNC and d.eng != "pe":
                    d.sig = True
        cnt = {}
        slotcnt = {}
        for o in ops:
            if o.slot is not None:
                slotcnt[o.slot] = slotcnt.get(o.slot, 0) + 16
                o.cnt = slotcnt[o.slot]
            elif o.sig:
                cnt[(o.eng, o.ep)] = cnt.get((o.eng, o.ep), 0) + 1
                o.cnt = cnt[(o.eng, o.ep)]
        st = self.stack
        esem = {k: st.enter_context(nc.semaphore("s_%s_%d" % k)) for k in cnt}
        ssem = {k: st.enter_context(nc.semaphore("d_" + str(k))) for k in slotcnt}
        block = st.enter_context(nc.Block())

        def run_engine(ename, eng):
            seen = {}
            for o in ops:
                if o.eng != ename:
                    continue
                need = {}
                for j in o.deps:
                    d = ops[j]
                    if d.slot is not None:
                        key, sem = ("slot", d.slot), ssem[d.slot]
                    else:
                        if d.eng == ename and o.slot is None:
                            if not (SAME_ENGINE_SYNC and ename != "pe"):
                                continue
                        key, sem = ("eng", d.eng, d.ep), esem[(d.eng, d.ep)]
                    if need.get(key, (None, 0))[1] < d.cnt:
                        need[key] = (sem, d.cnt)
                for key, (sem, c) in need.items():
                    if seen.get(key, 0) >= c:
                        continue
                    seen[key] = c
                    eng.wait_ge(sem, c)
                name, args, kwargs = o.fn
                ins = getattr(eng, name)(*args, **kwargs)
                if o.slot is not None:
                    ins.then_inc(ssem[o.slot], 16)
                elif o.sig:
                    ins.then_inc(esem[(ename, o.ep)], 1)
            last = {}
            for o in ops:
                if o.eng == ename and o.slot is not None:
                    last[o.slot] = max(last.get(o.slot, 0), o.cnt)
            for k, v in last.items():
                if seen.get(("slot", k), 0) < v:
                    eng.wait_ge(ssem[k], v)

        block.tensor(lambda e: run_engine("pe", e))
        block.scalar(lambda e: run_engine("act", e))
        block.vector(lambda e: run_engine("dve", e))
        block.gpsimd(lambda e: run_engine("pool", e))
        block.sync(lambda e: run_engine("sp", e))

    def close(self):
        self.stack.close()


DBG = set()
LAYERS = [("gqa", 0), ("conv", 0), ("mla", 0), ("gqa", 1)]


class Builder:
    def __init__(self, segs):
        self.segs = list(segs)
        self.nc = bass.Bass("TRN2", target_bir_lowering=False)
        self.P = Prog(self.nc)
        self.ext_in = []
        self.ext_out = []
        self.bank_i = 0
        self.wl_n = 0
        self.tmp_i = 0
        self.slot_n = {}
        self.ring = NSLOT

    def din(self, name, shape, dt=F32):
        self.ext_in.append(name)
        return self.nc.dram_tensor(name, list(shape), dt, kind="ExternalInput").ap()

    def dout(self, name, shape, dt=F32):
        self.ext_out.append(name)
        return self.nc.dram_tensor(name, list(shape), dt, kind="ExternalOutput").ap()

    def dstate(self, name, shape, dt, prod, cons):
        p, c = prod in self.segs, cons in self.segs
        if p and c:
            return self.nc.dram_tensor(name, list(shape), dt).ap()
        if p:
            return self.dout(name, shape, dt)
        if c:
            return self.din(name, shape, dt)
        return None

    def bank(self):
        b = self.bank_i % 8
        self.bank_i += 1
        return self.psb[b], "ps%d" % b

    def tmp(self):
        i = self.tmp_i % 3
        self.tmp_i += 1
        return self.tmpb[i], "tmp%d" % i

    def rot(self, key, n):
        i = self.slot_n.get(key, 0)
        self.slot_n[key] = i + 1
        return i % n

    def wload(self, src, shape):
        s = self.wl_n % self.ring
        self.wl_n += 1
        n = 1
        for d in shape[1:]:
            n *= d
        view = self.wring[s][:, 0:n]
        if len(shape) == 3:
            view = view.rearrange("p (a b) -> p a b", a=shape[1])
        elif len(shape) == 4:
            view = view.rearrange("p (a b c) -> p a b c", a=shape[1], b=shape[2])
        if len(shape) == 4:
            for a in range(shape[1]):
                self.P.dma("pool", view[:, a], src[:, a], slot="w%d" % s, writes=["w%d" % s])
        else:
            self.P.dma("pool", view, src, slot="w%d" % s, writes=["w%d" % s])
        return view, "w%d" % s

    def run_tasks(self, tasks, hold=0):
        lidx = [i for i, t in enumerate(tasks) if t[0] is not None]
        views = {}
        li = 0
        done = 0
        for i, (ls, comp) in enumerate(tasks):
            while li < len(lidx) and li < done + self.ring - hold:
                src, shape = tasks[lidx[li]][0]
                views[lidx[li]] = self.wload(src, shape)
                li += 1
            if ls is not None:
                comp(*views.pop(i))
                done += 1
            else:
                comp()

    def build(self):
        P, nc, segs = self.P, self.nc, self.segs
        first, last = segs[0], segs[-1]
        I = {}
        I["cmat"] = self.din("cmat", [128, 3, 128])
        I["ropetab"] = self.din("ropetab", [2, 128, 2, NL])
        I["n1T"] = self.din("n1T", [128, 4, 16])
        I["n2T"] = self.din("n2T", [128, 4, 16])
        self.mlp_layers = [l for l, sg in ((0, 1), (1, 2), (2, 3), (3, 4)) if sg in segs]
        if self.mlp_layers:
            I["w_mlp1"] = self.din("w_mlp1", [len(self.mlp_layers), D, 4 * D])
            I["w_mlp2"] = self.din("w_mlp2", [len(self.mlp_layers), 4 * D, D])
        if 0 in segs:
            I["xT"] = self.din("xT", [D, NT])
            I["cvec"] = self.din("cvec", [128, 16, 2])
            self.ada_shard = (segs == [0, 1, 2, 3, 4])
            if self.ada_shard:
                I["w_ada_sh"] = self.din("w_ada_sh", [4, D, 1536])
                I["b_adaT_sh"] = self.din("b_adaT_sh", [128, 4, 12])
            else:
                I["w_ada"] = self.din("w_ada", [4, D, 6 * D])
                I["b_adaT"] = self.din("b_adaT", [128, 4, 96])
        if first > 0:
            I["x_in"] = self.din("x_in", [128, KC, NT])
            I["mods_in"] = self.din("mods_in", [128, 4, 96, 2])
        if any(s in segs for s in (0, 1, 3, 4)):
            I["a_w_qkv"] = self.din("a_w_qkv", [2, D, 3072])
            I["a_w_o"] = self.din("a_w_o", [2, D, D])
            I["a_qk"] = self.din("a_qk", [128, 2, 2])
        if any(s in segs for s in (1, 2)):
            I["b_w_in"] = self.din("b_w_in", [D, 3 * D])
            I["b_convT"] = self.din("b_convT", [128, 3, 16])
            I["b_w_out"] = self.din("b_w_out", [D, D])
        if any(s in segs for s in (2, 3)):
            I["c_w_dq"] = self.din("c_w_dq", [D, 512])
            I["c_w_uq"] = self.din("c_w_uq", [512, 3072])
            I["c_w_dkv"] = self.din("c_w_dkv", [D, 576])
            I["c_w_ukv"] = self.din("c_w_ukv", [512, 4096])
            I["c_w_o"] = self.din("c_w_o", [D, D])
            I["c_nT"] = self.din("c_nT", [128, 2, 4])
        if 4 in segs:
            I["fnT"] = self.din("fnT", [128, 16])
        if 2 in segs:
            if 1 in segs:
                I["hmask"] = self.din("hmask", [128, 2, 8])
            else:
                I["halo_lr"] = self.din("halo_lr", [128, 16, 2])
        self.I = I
        Dm = {}
        kvrows = {0: 1024, 2: 4160, 3: 1024}
        qrows = {0: 16 * 128, 2: 16 * 128 + 16 * 64, 3: 16 * 128}
        for L, sp in ((0, 0), (2, 2), (3, 3)):
            Dm["qs%d" % L] = self.dstate("qs%d" % L, [qrows[L], NT], BF16, sp, sp + 1)
            Dm["kvc%d" % L] = self.dstate("kvc%d" % L, [kvrows[L], NCX], BF16, sp, sp + 1)
            if sp in segs and sp + 1 in segs:
                Dm["kvo%d" % L] = nc.dram_tensor("kvo%d" % L, [kvrows[L], NL], BF16).ap()
                Dm["kva%d" % L] = nc.dram_tensor("kva%d" % L, [8 * kvrows[L], NL], BF16).ap()
            elif sp in segs:
                Dm["kvo%d" % L] = self.dout("kvo%d" % L, [kvrows[L], NL], BF16)
            elif sp + 1 in segs:
                Dm["kva%d" % L] = self.din("kva%d" % L, [8 * kvrows[L], NL], BF16)
        if 1 in segs and 2 in segs:
            Dm["halo_o"] = nc.dram_tensor("halo_o", [128, 32], F32).ap()
            Dm["halo_a"] = nc.dram_tensor("halo_a", [8 * 128, 32], F32).ap()
        elif 1 in segs:
            Dm["halo_o"] = self.dout("halo_o", [128, 32], F32)
        Dm["bgh"] = self.dstate("bgh", [128, 16, 2], F32, 1, 2)
        self.Dm = Dm
        self.x = P.sb("x", [128, KC, NT], F32)
        self.h = P.sb("h", [128, KC * NT], BF16)
        self.aux = P.sb("aux", [128, 8 * NT], BF16)
        self.wring = [P.sb("wr%d" % i, [128, WSLOT if i == 3 else 4096], BF16) for i in range(NSLOT)]
        self.tab = P.sb("tab", [128, 2, NL], F32)
        self.ptb = [P.sb("pt%d" % i, [128, 512], BF16) for i in range(4)]
        self.tmpb = [P.sb("tmp%d" % i, [128, 512], F32) for i in range(3)]
        self.sqb = [P.sb("sq%d" % i, [128, 512], BF16) for i in range(2)]
        self.stg = [P.sb("stg%d" % i, [128, 512], BF16) for i in range(2)]
        self.rstd = P.sb("rstd", [128, NT], F32)
        self.mods = P.sb("mods", [128, 4, 96, 2], F32)
        self.vecA = P.sb("vecA", [128, 2, 16, 2], F32)
        self.cm = P.sb("cm", [128, 3, 128], BF16)
        self.nT = P.sb("nT", [128, 2, 4, 16], F32)
        self.small = P.sb("small", [128, 160], F32)
        self.silb = P.sb("silb", [128, 32], BF16)
        self.psb = [P.ps("psb%d" % i, [128, 512]) for i in range(8)]
        self.mbuf = self.tab[:, :, :].rearrange("p a t -> p (a t)")[:, 0:1288]
        tb16 = self.tab[:, :, :].rearrange("p a t -> p (a t)").bitcast(BF16)
        self.tpb = [tb16[:, 512 * i:512 * (i + 1)] for i in range(8)]
        self.TP = ["tp%d" % i for i in range(8)]
        self.ones = self.cm[:, 0, :]
        self.rot128 = self.cm[:, 1, :]
        self.rot64 = self.cm[:, 2, :]
        hv = self.h[:, :].rearrange("p (c t) -> p c t", c=KC)
        self.hv = hv
        P.dma("pool", self.cm[:], I["cmat"], slot="c0", writes=["cm"])
        P.dma("sp", self.nT[:, 0], I["n1T"], slot="c1", writes=["nT"])
        P.dma("sp", self.nT[:, 1], I["n2T"], slot="c1", writes=["nT"])
        if first == 0:
            xv = I["xT"].rearrange("(c p) t -> p c t", p=128)
            for c4 in range(4):
                P.dma("sp", self.x[:, 4 * c4:4 * c4 + 4, :], xv[:, 4 * c4:4 * c4 + 4, :], slot="xl",
                      writes=["x%d.%d" % (c, t) for c in range(4 * c4, 4 * c4 + 4) for t in range(3)])
            if self.ada_shard:
                self.stage_ada_sharded()
            else:
                self.stage_ada()
        else:
            for c4 in range(4):
                P.dma("sp", self.x[:, 4 * c4:4 * c4 + 4, :], I["x_in"][:, 4 * c4:4 * c4 + 4, :], slot="xl",
                      writes=["x%d.%d" % (c, t) for c in range(4 * c4, 4 * c4 + 4) for t in range(3)])
            P.dma("sp", self.mods[:], I["mods_in"], slot="c1", writes=["mods"])
        for s in segs:
            P.epoch = s
            getattr(self, "seg%d" % s)()
            if "dumpx" in DBG and s < 4:
                xd = self.dout("xdbg%d" % s, [128, KC, NT])
                for c4 in range(4):
                    P.dma("sp", xd[:, 4 * c4:4 * c4 + 4, :], self.x[:, 4 * c4:4 * c4 + 4, :], slot="xs",
                          reads=["x%d.%d" % (c, t) for c in range(4 * c4, 4 * c4 + 4) for t in range(3)])
                if s == 0:
                    md = self.dout("modsdbg", [128, 4, 96, 2])
                    P.dma("sp", md, self.mods[:], slot="xs", reads=["mods"])
        if last < 4:
            xo = self.dout("x_out", [128, KC, NT])
            mo = self.dout("mods_out", [128, 4, 96, 2])
            for c4 in range(4):
                P.dma("sp", xo[:, 4 * c4:4 * c4 + 4, :], self.x[:, 4 * c4:4 * c4 + 4, :], slot="xs",
                      reads=["x%d.%d" % (c, t) for c in range(4 * c4, 4 * c4 + 4) for t in range(3)])
            P.dma("sp", mo, self.mods[:], slot="xs", reads=["mods"])
        P.build()
        P.close()
        return nc

    def stage_ada(self):
        P, I = self.P, self.I
        cv = self.small[:, 0:32].rearrange("p (k m) -> p k m", m=2)
        sil = self.silb[:, :].rearrange("p (k m) -> p k m", m=2)
        bt = self.small[:, 32:128]
        P.dma("sp", cv, I["cvec"], slot="c1", writes=["small"])
        P.op("act", lambda e: e.activation(out=sil, in_=cv, func=AF.Silu), reads=["small"], writes=["sil"])
        tasks = []
        for l in range(4):
            wv = I["w_ada"][l].rearrange("(c p) n -> p c n", p=128)
            ps, psn = self.bank()

            def comp(view, wres, l=l, ps=ps, psn=psn, j0=0):
                for jj in range(2):
                    j = j0 + jj
                    for k in range(KC):
                        P.op("pe", lambda e, k=k, jj=jj, j=j: e.matmul(
                            ps[:, 2 * j:2 * j + 2], view[:, k, jj * 128:(jj + 1) * 128], sil[:, k, :],
                            start=(k == 0), stop=(k == KC - 1)), reads=[wres, "sil"], writes=[psn])

            for jp in range(48):
                tasks.append(((wv[:, :, jp * 256:(jp + 1) * 256], [128, KC, 256]),
                              lambda v, r, comp=comp, jp=jp: comp(v, r, j0=2 * jp)))

            def fin(l=l, ps=ps, psn=psn):
                P.dma("sp", bt, I["b_adaT"][:, l, :], slot="c2", writes=["bt"])
                pv = ps[:, 0:192].rearrange("p (j m) -> p j m", m=2)
                for m in range(2):
                    P.op("dve", lambda e, m=m: e.tensor_tensor(out=self.mods[:, l, :, m], in0=pv[:, :, m], in1=bt,
                                                              op=ALU.add), reads=[psn, "bt"], writes=["mods"])
            tasks.append((None, fin))
        self.run_tasks(tasks)

    def stage_ada_sharded(self):
        P, I, nc = self.P, self.I, self.nc
        cv = self.small[:, 0:32].rearrange("p (k m) -> p k m", m=2)
        sil = self.silb[:, :].rearrange("p (k m) -> p k m", m=2)
        bt = self.small[:, 32:80].rearrange("p (l j) -> p l j", l=4)
        mo = P.sb("mown", [128, 4, 12, 2], F32)
        own = nc.dram_tensor("mods_own", [128, 96], F32).ap()
        allm = nc.dram_tensor("mods_all", [8 * 128, 96], F32).ap()
        P.dma("sp", cv, I["cvec"], slot="c1", writes=["small"])
        P.dma("sp", bt, I["b_adaT_sh"], slot="c1", writes=["bt"])
        P.op("act", lambda e: e.activation(out=sil, in_=cv, func=AF.Silu), reads=["small"], writes=["sil"])
        ps, psn = self.bank()
        tasks = []
        for l in range(4):
            wv = I["w_ada_sh"][l].rearrange("(c p) n -> p c n", p=128)
            for jp in range(6):
                def comp(view, wres, l=l, jp=jp):
                    for jj in range(2):
                        col = 2 * (l * 12 + 2 * jp + jj)
                        for k in range(KC):
                            P.op("pe", lambda e, k=k, jj=jj, col=col: e.matmul(
                                ps[:, col:col + 2], view[:, k, jj * 128:(jj + 1) * 128], sil[:, k, :],
                                start=(k == 0), stop=(k == KC - 1)), reads=[wres, "sil"], writes=[psn])
                tasks.append(((wv[:, :, jp * 256:(jp + 1) * 256], [128, KC, 256]), comp))
        self.run_tasks(tasks)
        pv = ps[:, 0:96].rearrange("p (l j m) -> p l j m", l=4, m=2)
        for m in range(2):
            P.op("dve", lambda e, m=m: e.tensor_tensor(out=mo[:, :, :, m], in0=pv[:, :, :, m], in1=bt, op=ALU.add),
                 reads=[psn, "bt"], writes=["mown"])
        P.dma("sp", own, mo[:].rearrange("p l j m -> p (l j m)"), slot="c2", reads=["mown"], writes=["dram_mown"])
        P.op("pool", lambda e: e.collective_compute("AllGather", ALU.bypass, replica_groups=[list(range(NCORES))],
                                                    ins=[own.opt()], outs=[allm.opt()]),
             reads=["dram_mown"], writes=["dram_mall"])
        for r in range(NCORES):
            for l in range(4):
                P.dma("sp", self.mods[:, l, r * 12:(r + 1) * 12, :],
                      allm[r * 128:(r + 1) * 128, l * 24:(l + 1) * 24].rearrange("p (j m) -> p j m", m=2),
                      slot="c2", reads=["dram_mall"], writes=["mods"])

    def layer_vecs(self, l):
        P = self.P
        for which, joff in ((0, 16), (1, 64)):
            for m in range(2):
                P.op("dve", lambda e, which=which, joff=joff, m=m: e.scalar_tensor_tensor(
                    out=self.vecA[:, which, :, m], in0=self.mods[:, l, joff:joff + 16, m], scalar=1.0,
                    in1=self.nT[:, which, l, :], op0=ALU.add, op1=ALU.mult),
                    reads=["mods", "nT"], writes=["vecA"])

    def rsqrt_inplace(self, ap, res):
        self.P.op("dve", lambda e: e.reciprocal(out=ap, in_=ap), reads=[res], writes=[res])
        self.P.op("act", lambda e: e.activation(out=ap, in_=ap, func=AF.Sqrt), reads=[res], writes=[res])

    def modv(self, l, j, c, m):
        return self.mods[:, l, j * 16 + c, m:m + 1]

    def stage_norm(self, l, which, tiles, final=False, outT=None):
        P = self.P
        for t in tiles:
            t0, tw = TILES[t]
            m = 1 if t == 2 else 0
            ps, psn = self.bank()
            for c in range(KC):
                si = self.rot("sq", 2)
                sq = self.sqb[si]
                P.op("act", lambda e, c=c, sq=sq: e.activation(out=sq[:, :tw], in_=self.x[:, c, t0:t0 + tw],
                                                              func=AF.Square),
                     reads=["x%d.%d" % (c, t)], writes=["sq%d" % si])
                P.op("pe", lambda e, c=c, sq=sq: e.matmul(ps[:, :tw], self.ones, sq[:, :tw], start=(c == 0),
                                                         stop=(c == KC - 1)),
                     reads=["sq%d" % si, "cm"], writes=[psn])
            rs = self.rstd[:, t0:t0 + tw]
            P.op("dve", lambda e: e.tensor_scalar(out=rs, in0=ps[:, :tw], scalar1=1.0 / D, scalar2=EPS,
                                                 op0=ALU.mult, op1=ALU.add), reads=[psn], writes=["rstd%d" % t])
            self.rsqrt_inplace(rs, "rstd%d" % t)
            for c in range(KC):
                tb, tn = self.tmp()
                if final:
                    A = self.I_fn[:, c:c + 1]
                else:
                    A = self.vecA[:, which, c, m:m + 1]
                P.op("dve", lambda e, c=c, tb=tb, A=A: e.scalar_tensor_tensor(
                    out=tb[:, :tw], in0=self.x[:, c, t0:t0 + tw], scalar=A, in1=rs, op0=ALU.mult, op1=ALU.mult),
                    reads=["x%d.%d" % (c, t), "rstd%d" % t, "vecA", "fn"], writes=[tn])
                if final:
                    P.dma("sp", outT[:, c, t0:t0 + tw], tb[:, :tw], slot="o%d" % (self.rot("oslot", 4)), reads=[tn])
                else:
                    B = self.modv(l, 0 if which == 0 else 3, c, m)
                    P.op("act", lambda e, c=c, tb=tb, B=B: e.activation(
                        out=self.hv[:, c, t0:t0 + tw], in_=tb[:, :tw], func=AF.Identity, bias=B, scale=1.0),
                        reads=[tn, "mods"], writes=["h%d.%d" % (c, t)])

    def x_accum(self, l, gj, oc, t, ps, psn, tw):
        t0, _ = TILES[t]
        m = 1 if t == 2 else 0
        G = self.modv(l, gj, oc, m)
        self.P.op("dve", lambda e: e.scalar_tensor_tensor(
            out=self.x[:, oc, t0:t0 + tw], in0=ps[:, :tw], scalar=G, in1=self.x[:, oc, t0:t0 + tw],
            op0=ALU.mult, op1=ALU.add), reads=[psn, "mods", "x%d.%d" % (oc, t)], writes=["x%d.%d" % (oc, t)])

    def stage_mlp(self, l, tiles):
        P, I = self.P, self.I
        self.stage_norm(l, 1, tiles)
        h1 = self.aux[:, :].rearrange("p (c t) -> p c t", c=8)
        li = self.mlp_layers.index(l)
        w1 = I["w_mlp1"][li].rearrange("(c p) n -> p c n", p=128)
        w2 = I["w_mlp2"][li].rearrange("(c p) n -> p c n", p=128)
        tasks = []
        for g in range(8):
            for pp in range(4):
                def c1(view, wres, pp=pp):
                    for fi in range(2):
                        f = 2 * pp + fi
                        for t in tiles:
                            t0, tw = TILES[t]
                            ps, psn = self.bank()
                            for k in range(KC):
                                P.op("pe", lambda e, k=k, fi=fi, ps=ps: e.matmul(
                                    ps[:, :tw], view[:, k, fi * 128:(fi + 1) * 128], self.hv[:, k, t0:t0 + tw],
                                    start=(k == 0), stop=(k == KC - 1)),
                                    reads=[wres, "h%d.%d" % (k, t)], writes=[psn])
                            tb, tn = self.tmp()
                            P.op("act", lambda e, ps=ps, tb=tb: e.activation(out=tb[:, :tw], in_=ps[:, :tw],
                                                                          func=AF.Relu), reads=[psn], writes=[tn])
                            P.op("pool", lambda e, tb=tb, f=f: e.tensor_tensor(
                                out=h1[:, f, t0:t0 + tw], in0=tb[:, :tw], in1=tb[:, :tw], op=ALU.mult),
                                reads=[tn], writes=["a%d.%d" % (f, t)])
                col = g * 1024 + pp * 256
                tasks.append(((w1[:, :, col:col + 256], [128, KC, 256]), c1))
            for op4 in range(4):
                def c2(view, wres, op4=op4):
                    for oi in range(4):
                        oc = 4 * op4 + oi
                        for t in tiles:
                            t0, tw = TILES[t]
                            ps, psn = self.bank()
                            for kk in range(8):
                                P.op("pe", lambda e, kk=kk, oi=oi, ps=ps: e.matmul(
                                    ps[:, :tw], view[:, kk, oi * 128:(oi + 1) * 128], h1[:, kk, t0:t0 + tw],
                                    start=(kk == 0), stop=(kk == 7)),
                                    reads=[wres, "a%d.%d" % (kk, t)], writes=[psn])
                            self.x_accum(l, 5, oc, t, ps, psn, tw)
                tasks.append(((w2[:, 8 * g:8 * g + 8, op4 * 512:(op4 + 1) * 512], [128, 8, 512]), c2))
        self.run_tasks(tasks)

    def stage_tables(self, blk):
        self.P.dma("sp", self.tab[:], self.I["ropetab"][0 if blk == 128 else 1], slot="c1", reads=["tab"],
                   writes=["tab"] + self.TP)

    def head_post(self, ps, psn, t, rows, gain, norm_n, rotm, dst, rope=True, dres="dram_kv"):
        P = self.P
        t0, tw = TILES[t]
        latent = (t != 2) and rope
        pv = ps[0:rows, :tw]
        if gain is None and norm_n is None and not latent:
            gi = self.rot("stg", 2)
            sg = self.stg[gi]
            P.op("act", lambda e: e.activation(out=sg[0:rows, :tw], in_=pv, func=AF.Copy), reads=[psn],
                 writes=["stg%d" % gi])
            P.dma("sp", dst, sg[0:rows, :tw], slot="st%d" % gi, reads=["stg%d" % gi], writes=[dres])
            return
        qg, qgn = self.tmp()
        if gain is not None:
            P.op("act", lambda e: e.activation(out=qg[0:rows, :tw], in_=pv, func=AF.Copy, scale=gain),
                 reads=[psn, "small"], writes=[qgn])
        else:
            P.op("act", lambda e: e.activation(out=qg[0:rows, :tw], in_=pv, func=AF.Copy),
                 reads=[psn], writes=[qgn])
        rsn = None
        if norm_n is not None:
            si = self.rot("sq", 2)
            sq = self.sqb[si]
            P.op("act", lambda e: e.activation(out=sq[0:rows, :tw], in_=pv, func=AF.Square),
                 reads=[psn], writes=["sq%d" % si])
            p2, p2n = self.bank()
            P.op("pe", lambda e: e.matmul(p2[0:rows, :tw], self.ones[0:rows, 0:rows], sq[0:rows, :tw],
                                          start=True, stop=True), reads=["sq%d" % si, "cm"], writes=[p2n])
            rsb, rsn = self.tmp()
            P.op("dve", lambda e: e.tensor_scalar(out=rsb[0:rows, :tw], in0=p2[0:rows, :tw], scalar1=1.0 / norm_n,
                                                 scalar2=EPS, op0=ALU.mult, op1=ALU.add), reads=[p2n], writes=[rsn])
            self.rsqrt_inplace(rsb[0:rows, :tw], rsn)
        gi = self.rot("stg", 2)
        sg = self.stg[gi]
        sgn = "stg%d" % gi
        if latent:
            bi = self.rot("pt", 4)
            qb = self.ptb[bi]
            P.op("pool", lambda e: e.tensor_copy(out=qb[0:rows, :tw], in_=qg[0:rows, :tw]),
                 reads=[qgn], writes=["pt%d" % bi])
            p3, p3n = self.bank()
            P.op("pe", lambda e: e.matmul(p3[0:rows, :tw], rotm[0:rows, 0:rows], qb[0:rows, :tw],
                                          start=True, stop=True), reads=["pt%d" % bi, "cm"], writes=[p3n])
            t2, t2n = self.tmp()
            P.op("dve", lambda e: e.tensor_tensor(out=t2[0:rows, :tw], in0=p3[0:rows, :tw],
                                                 in1=self.tab[0:rows, 1, t0:t0 + tw], op=ALU.mult),
                 reads=[p3n, "tab"], writes=[t2n])
            P.op("dve", lambda e: e.tensor_tensor(out=qg[0:rows, :tw], in0=qg[0:rows, :tw],
                                                 in1=self.tab[0:rows, 0, t0:t0 + tw], op=ALU.mult),
                 reads=[qgn, "tab"], writes=[qgn])
            if rsn is not None:
                P.op("dve", lambda e: e.tensor_tensor(out=qg[0:rows, :tw], in0=qg[0:rows, :tw], in1=t2[0:rows, :tw],
                                                     op=ALU.add), reads=[qgn, t2n], writes=[qgn])
                P.op("dve", lambda e: e.tensor_tensor(out=sg[0:rows, :tw], in0=qg[0:rows, :tw], in1=rsb[0:rows, :tw],
                                                     op=ALU.mult), reads=[qgn, rsn], writes=[sgn])
            else:
                P.op("dve", lambda e: e.tensor_tensor(out=sg[0:rows, :tw], in0=qg[0:rows, :tw], in1=t2[0:rows, :tw],
                                                     op=ALU.add), reads=[qgn, t2n], writes=[sgn])
        else:
            if rsn is not None:
                P.op("dve", lambda e: e.tensor_tensor(out=sg[0:rows, :tw], in0=qg[0:rows, :tw], in1=rsb[0:rows, :tw],
                                                     op=ALU.mult), reads=[qgn, rsn], writes=[sgn])
            else:
                P.op("dve", lambda e: e.tensor_copy(out=sg[0:rows, :tw], in_=qg[0:rows, :tw]),
                     reads=[qgn], writes=[sgn])
        P.dma("sp", dst, sg[0:rows, :tw], slot="st%d" % gi, reads=[sgn], writes=[dres])

    def v_tiles(self, view, wres, src, srcname, nk, ncols, dst_fn):
        P = self.P
        for jt in range(10):
            t = 0 if jt < 4 else (1 if jt < 8 else 2)
            ps, psn = self.bank()
            for k in range(nk):
                P.op("pe", lambda e, k=k, ps=ps: e.matmul(ps[:, :ncols], src(k, jt), view[:, k, 0:ncols],
                                                         start=(k == 0), stop=(k == nk - 1)),
                     reads=[wres, srcname(k, t)], writes=[psn])
            gi = self.rot("stg", 2)
            sg = self.stg[gi]
            P.op("act", lambda e, ps=ps, sg=sg: e.activation(out=sg[:, :ncols], in_=ps[:, :ncols], func=AF.Copy),
                 reads=[psn], writes=["stg%d" % gi])
            P.dma("sp", dst_fn(jt), sg[:, :ncols].rearrange("p (g d) -> p g d", d=128), slot="st%d" % gi,
                  reads=["stg%d" % gi], writes=["dram_kv"])

    def stage_gqa_proj(self, l, j, L, ctx_q, exch=None):
        P, I, Dm = self.P, self.I, self.Dm
        self.stage_tables(128)
        gq = self.small[:, 128:130]
        P.dma("sp", gq, I["a_qk"][:, j, :], slot="c2", reads=["small"], writes=["small"])
        wv = I["a_w_qkv"][j].rearrange("(c p) n -> p c n", p=128)
        kvo, kvc, qs = Dm["kvo%d" % L], Dm["kvc%d" % L], Dm["qs%d" % L]
        tasks = []

        def proj_heads(view, wres, heads, gain, dst_of, tiles, dres):
            for hi, hd in enumerate(heads):
                for t in tiles:
                    t0, tw = TILES[t]
                    ps, psn = self.bank()
                    for k in range(KC):
                        P.op("pe", lambda e, k=k, ps=ps, hi=hi: e.matmul(
                            ps[:, :tw], view[:, k, hi * 128:(hi + 1) * 128], self.hv[:, k, t0:t0 + tw],
                            start=(k == 0), stop=(k == KC - 1)), reads=[wres, "h%d.%d" % (k, t)], writes=[psn])
                    self.head_post(ps, psn, t, 128, gain, 128, self.rot128, dst_of(hd, t), dres=dres)

        def kdst(g, t):
            t0, tw = TILES[t]
            if t == 2:
                return kvc[g * 128:(g + 1) * 128, :]
            return kvo[g * 128:(g + 1) * 128, t0:t0 + tw]

        def qdst(hd, t):
            t0, tw = TILES[t]
            return qs[hd * 128:(hd + 1) * 128, t0:t0 + tw]

        for kp in range(2):
            tasks.append(((wv[:, :, 2048 + kp * 256:2048 + (kp + 1) * 256], [128, KC, 256]),
                          lambda v, r, kp=kp: proj_heads(v, r, [2 * kp, 2 * kp + 1], gq[:, 1:2], kdst, [0, 1, 2], "dram_kv")))
        for vp in range(2):
            def vdst(jt, vp=vp):
                if jt < 8:
                    return kvo[512 + vp * 256:512 + (vp + 1) * 256, jt * 128:(jt + 1) * 128].rearrange(
                        "(g p) d -> p g d", p=128)
                return kvc[512 + vp * 256:512 + (vp + 1) * 256, (jt - 8) * 128:(jt - 7) * 128].rearrange(
                    "(g p) d -> p g d", p=128)
            tasks.append(((wv[:, :, 2560 + vp * 256:2560 + (vp + 1) * 256], [128, KC, 256]),
                          lambda v, r, vdst=vdst: self.v_tiles(
                              v, r, lambda k, jt: self.hv[:, k, jt * 128:(jt + 1) * 128],
                              lambda k, t: "h%d.%d" % (k, t), KC, 256, vdst)))
        if exch is not None:
            tasks.append((None, exch))
        qt = [0, 1, 2] if ctx_q else [0, 1]
        for qp in range(8):
            tasks.append(((wv[:, :, qp * 256:(qp + 1) * 256], [128, KC, 256]),
                          lambda v, r, qp=qp: proj_heads(v, r, [2 * qp, 2 * qp + 1], gq[:, 0:1], qdst, qt, "dram_q")))
        self.run_tasks(tasks)

    def exchange(self, own, allb, name):
        P = self.P
        P.op("pool", lambda e: e.collective_compute("AllGather", ALU.bypass, replica_groups=[list(range(NCORES))],
                                                    ins=[own.opt()], outs=[allb.opt()]),
             reads=["dram_kv", "dram_halo"], writes=["dram_all_" + name])

    @staticmethod
    def kvh(ck):
        return "kvA" if ck < 32 else "kvB"

    def kv_load(self, Kb, Vb, kva3, kvc, kr0, vr0, L):
        P = self.P
        hres = ["h%d.%d" % (c, t) for c in range(KC) for t in range(3)]
        ra = ["dram_all_kv%d" % L]
        wa, wb = hres + ["kvA"], hres + ["kvB"]
        P.dma("sp", Kb[:, 0:4096].rearrange("p (r t) -> p r t", r=4), kva3[kr0:kr0 + 128, 0:4, :], slot="kvA",
              reads=ra, writes=wa)
        P.dma("sp", Vb[:, 0:32, :].rearrange("p (r j) d -> p r (j d)", r=4), kva3[vr0:vr0 + 128, 0:4, :], slot="kvA",
              reads=ra, writes=wa)
        P.dma("sp", Kb[:, 4096:8192].rearrange("p (r t) -> p r t", r=4), kva3[kr0:kr0 + 128, 4:8, :], slot="kvB",
              reads=ra, writes=wb)
        P.dma("sp", Kb[:, 8192:NKEY], kvc[kr0:kr0 + 128, :], slot="kvB", reads=["dram_kv"], writes=wb)
        P.dma("sp", Vb[:, 32:64, :].rearrange("p (r j) d -> p r (j d)", r=4), kva3[vr0:vr0 + 128, 4:8, :], slot="kvB",
              reads=ra, writes=wb)
        P.dma("sp", Vb[:, 64:66, :].rearrange("p j d -> p (j d)"), kvc[vr0:vr0 + 128, :], slot="kvB",
              reads=["dram_kv"], writes=wb)

    def attn_unit(self, q0, qw, chunks, kq, vch, scale, out_ap, extra_s, out_res):
        P = self.P
        SB = (self.psb[0], self.psb[1], self.psb[2], self.psb[7])
        SBI = (0, 1, 2, 7)
        oi = 3 + 2 * self.rot("ob", 2)
        psO, psS = self.psb[oi], self.psb[oi + 1]
        n = len(chunks)
        assert n % 2 == 0
        ptl = [(self.ptb[i], "pt%d" % i) for i in range(4)] + [(self.tpb[i], "tp%d" % i) for i in range(2)]
        sml = [(self.tpb[2 + i], "tp%d" % (2 + i)) for i in range(6)]

        def emit_s(i):
            bi = self.rot("sb", 4)
            prs = kq(chunks[i])
            for pi, (lt, rh) in enumerate(prs):
                P.op("pe", lambda e, lt=lt, rh=rh, pi=pi: e.matmul(
                    SB[bi][:, :qw], lt, rh, start=(pi == 0), stop=(pi == len(prs) - 1)),
                    reads=[self.kvh(chunks[i])] + extra_s, writes=["ps%d" % SBI[bi]])
            return bi

        groups = [list(range(g0, min(g0 + 8, n))) for g0 in range(0, n, 8)]
        ng = len(groups)
        sums = {}

        def add(eng, dst, a_, b_):
            (d, dn), (x_, xn), (y_, yn) = dst, a_, b_
            P.op(eng, lambda e: e.tensor_tensor(out=d[:, :qw], in0=x_[:, :qw], in1=y_[:, :qw], op=ALU.add),
                 reads=[xn, yn], writes=[dn])

        def after_exp(i):
            gi, k = i // 8, i % 8
            idx = groups[gi]
            if k % 2 == 1:
                t = sml[self.rot("asm", 6)]
                sums[(gi, k // 2)] = t
                add("dve" if k < 4 else "pool", t, pts[i - 1], pts[i])
                if k % 4 == 3:
                    lo = sums[(gi, k // 2 - 1)]
                    add("dve" if k < 4 else "pool", lo, lo, t)
                    if k == 7:
                        add("dve", sums[(gi, 0)], sums[(gi, 0)], sums[(gi, 2)])
            if i == idx[-1]:
                m = len(idx)
                if m == 2 or m == 4 or m == 8:
                    pass
                elif m == 6:
                    add("dve", sums[(gi, 0)], sums[(gi, 0)], sums[(gi, 2)])
                else:
                    raise AssertionError(m)

        def emit_sum(gi):
            sa, san = sums[(gi, 0)]
            P.op("pe", lambda e: e.matmul(psS[:, :qw], self.ones, sa[:, :qw], start=(gi == 0), stop=(gi == ng - 1)),
                 reads=[san, "cm"], writes=["ps%d" % (oi + 1)])

        sbk = {}
        pts = {}
        for i in range(min(3, n)):
            sbk[i] = emit_s(i)
        gdone = 0
        for i in range(n):
            bi = sbk.pop(i)
            pt, ptn = ptl[self.rot("apt", 6)]
            pts[i] = (pt, ptn)
            P.op("act", lambda e, pt=pt: e.activation(out=pt[:, :qw], in_=SB[bi][:, :qw], func=AF.Exp, scale=scale),
                 reads=["ps%d" % SBI[bi]], writes=[ptn])
            P.op("pe", lambda e, pt=pt: e.matmul(psO[:, :qw], vch(chunks[i]), pt[:, :qw], start=(i == 0),
                                                 stop=(i == n - 1)),
                 reads=[self.kvh(chunks[i]), ptn], writes=["ps%d" % oi])
            after_exp(i)
            if i + 3 < n:
                sbk[i + 3] = emit_s(i + 3)
            if gdone < ng and i >= groups[gdone][-1] + 1:
                emit_sum(gdone)
                gdone += 1
        while gdone < ng:
            emit_sum(gdone)
            gdone += 1
        rc, rcn = self.tmp()
        P.op("dve", lambda e: e.reciprocal(out=rc[:, :qw], in_=psS[:, :qw]), reads=["ps%d" % (oi + 1)], writes=[rcn])
        P.op("dve", lambda e: e.tensor_tensor(out=out_ap, in0=psO[:, :qw], in1=rc[:, :qw], op=ALU.mult),
             reads=["ps%d" % oi, rcn], writes=[out_res])

    def wo_tasks(self, l, wo, g, tiles, Og):
        P = self.P
        tasks = []
        wv = wo.rearrange("(c p) n -> p c n", p=128)
        for half in range(2):
            def comp(view, wres, half=half):
                for oi in range(8):
                    oc = 8 * half + oi
                    for t in tiles:
                        t0, tw = TILES[t]
                        bi = (0, 1, 2, 7)[self.rot("wob", 4)]
                        ps, psn = self.psb[bi], "ps%d" % bi
                        for kk in range(4):
                            P.op("pe", lambda e, kk=kk, oi=oi, ps=ps: e.matmul(
                                ps[:, :tw], view[:, kk, oi * 128:(oi + 1) * 128], Og[:, kk, t0:t0 + tw],
                                start=(kk == 0), stop=(kk == 3)), reads=[wres, "ao"], writes=[psn])
                        self.x_accum(l, 2, oc, t, ps, psn, tw)
            tasks.append(((wv[:, 4 * g:4 * g + 4, half * 1024:(half + 1) * 1024], [128, 4, 1024]), comp))
        return tasks

    def stage_gqa_attn(self, l, j, L, ctx_out):
        P, I, Dm = self.P, self.I, self.Dm
        kva, kvc, qs = Dm["kva%d" % L], Dm["kvc%d" % L], Dm["qs%d" % L]
        Kb = self.h[:, 0:NKEY]
        Vb = self.h[:, NKEY:2 * NKEY].rearrange("p (c d) -> p c d", d=128)
        A4 = self.aux[:, :].rearrange("p (c t) -> p c t", c=8)
        Qg, Og = A4[:, 0:4, :], A4[:, 4:8, :]
        kva3 = kva.rearrange("(r n) t -> n r t", r=8)
        scale = 128 ** -0.5
        hres = ["h%d.%d" % (c, t) for c in range(KC) for t in range(3)]
        tiles = [0, 1, 2] if ctx_out else [0, 1]
        for g in range(4):
            self.kv_load(Kb, Vb, kva3, kvc, g * 128, 512 + g * 128, L)
            P.dma("sp", Qg, qs[g * 512:(g + 1) * 512, :].rearrange("(h p) t -> p h t", p=128), slot="q",
                  reads=["dram_q"], writes=["aq"])
            for hh in range(4):
                for t in tiles:
                    t0, tw = TILES[t]
                    chunks = list(range(NCH)) if t != 2 else [64, 65]
                    self.attn_unit(
                        t0, tw, chunks,
                        lambda ck, hh=hh, t0=t0, tw=tw: [(Kb[:, ck * 128:(ck + 1) * 128], Qg[:, hh, t0:t0 + tw])],
                        lambda ck: Vb[:, ck, :], scale, Og[:, hh, t0:t0 + tw], ["aq"], "ao")
            self.run_tasks(self.wo_tasks(l, I["a_w_o"][j], g, tiles, Og))

    def stage_conv_main(self, l):
        P, I, Dm = self.P, self.I, self.Dm
        win = I["b_w_in"].rearrange("(c p) n -> p c n", p=128)
        wout = I["b_w_out"].rearrange("(c p) n -> p c n", p=128)
        cw = self.small[:, 64:112].rearrange("p (w c) -> p w c", w=3)
        P.dma("sp", cw, I["b_convT"], slot="c2", reads=["small"], writes=["small"])
        mb = self.mbuf
        P.op("pool", lambda e: e.memset(mb[:, :], 0.0), reads=["tab"], writes=["tab"] + self.TP)
        bgh = self.small[:, 0:32].rearrange("p (c m) -> p c m", m=2)
        mh = self.small[:, 32:64].rearrange("p (c m) -> p c m", m=2)
        gb = self.aux[:, 0:2 * NT].rearrange("p (c t) -> p c t", c=2)
        moff = [1, 1, 3]
        stash = {}

        def m_compute(c, vcg, rcg, vu, ru):
            for t in range(3):
                t0, tw = TILES[t]
                pa, pan = self.bank()
                for k in range(KC):
                    P.op("pe", lambda e, k=k, pa=pa: e.matmul(pa[:, :tw], vcg[:, k, :], self.hv[:, k, t0:t0 + tw],
                                                             start=(k == 0), stop=(k == KC - 1)),
                         reads=[rcg, "h%d.%d" % (k, t)], writes=[pan])
                tb, tn = self.tmp()
                P.op("act", lambda e, pa=pa, tb=tb: e.activation(out=tb[:, :tw], in_=pa[:, :tw], func=AF.Copy),
                     reads=[pan], writes=[tn])
                pb, pbn = self.bank()
                for k in range(KC):
                    P.op("pe", lambda e, k=k, pb=pb: e.matmul(pb[:, :tw], vu[:, k, :], self.hv[:, k, t0:t0 + tw],
                                                             start=(k == 0), stop=(k == KC - 1)),
                         reads=[ru, "h%d.%d" % (k, t)], writes=[pbn])
                o = moff[t] + t0
                P.op("dve", lambda e, pb=pb, tb=tb, o=o: e.tensor_tensor(out=mb[:, o:o + tw], in0=pb[:, :tw],
                                                                        in1=tb[:, :tw], op=ALU.mult),
                     reads=[pbn, tn], writes=["tab"])
            if "c_notiny" in DBG:
                return
            P.op("dve", lambda e: e.tensor_copy(out=mh[:, c, 0:1], in_=mb[:, 1:2]), reads=["tab"], writes=["small"])
            P.op("dve", lambda e: e.tensor_copy(out=mh[:, c, 1:2], in_=mb[:, 1024:1025]), reads=["tab"],
                 writes=["small"])

        def gate(c, vbg, rbg):
            if "c_nogate" in DBG:
                return
            for t in range(3):
                t0, tw = TILES[t]
                pc, pcn = self.bank()
                for k in range(KC):
                    P.op("pe", lambda e, k=k, pc=pc: e.matmul(pc[:, :tw], vbg[:, k, :], self.hv[:, k, t0:t0 + tw],
                                                             start=(k == 0), stop=(k == KC - 1)),
                         reads=[rbg, "h%d.%d" % (k, t)], writes=[pcn])
                o = moff[t] + t0
                cvb, cvn = self.tmp()
                P.op("dve", lambda e, cvb=cvb, o=o: e.tensor_scalar(
                    out=cvb[:, :tw], in0=mb[:, o - 1:o - 1 + tw], scalar1=cw[:, 0, c:c + 1], scalar2=None,
                    op0=ALU.mult), reads=["tab", "small"], writes=[cvn])
                for wi in (1, 2):
                    P.op("dve", lambda e, cvb=cvb, o=o, wi=wi: e.scalar_tensor_tensor(
                        out=cvb[:, :tw], in0=mb[:, o - 1 + wi:o - 1 + wi + tw], scalar=cw[:, wi, c:c + 1],
                        in1=cvb[:, :tw], op0=ALU.mult, op1=ALU.add), reads=["tab", "small", cvn], writes=[cvn])
                P.op("dve", lambda e, cvb=cvb, pc=pc: e.tensor_tensor(
                    out=gb[:, c % 2, t0:t0 + tw], in0=pc[:, :tw], in1=cvb[:, :tw], op=ALU.mult),
                    reads=[pcn, cvn], writes=["ao"])
                if "c_notiny" in DBG:
                    continue
                if t == 0:
                    P.op("dve", lambda e, pc=pc: e.tensor_copy(out=bgh[:, c, 0:1], in_=pc[:, 0:1]),
                         reads=[pcn], writes=["small"])
                if t == 1:
                    P.op("dve", lambda e, pc=pc: e.tensor_copy(out=bgh[:, c, 1:2], in_=pc[:, 511:512]),
                         reads=[pcn], writes=["small"])

        def outp(half, view, wres):
            if "c_noout" in DBG:
                return
            for oi in range(8):
                oc = 8 * half + oi
                for t in range(3):
                    t0, tw = TILES[t]
                    ps, psn = self.bank()
                    for kk in range(2):
                        P.op("pe", lambda e, kk=kk, oi=oi, ps=ps: e.matmul(
                            ps[:, :tw], view[:, kk, oi * 128:(oi + 1) * 128], gb[:, kk, t0:t0 + tw],
                            start=(kk == 0), stop=(kk == 1)), reads=[wres, "ao"], writes=[psn])
                    self.x_accum(l, 2, oc, t, ps, psn, tw)

        tasks = []
        for c in range(KC):
            tasks.append(((win[:, :, 2048 + c * 128:2048 + (c + 1) * 128], [128, KC, 128]),
                          lambda v, r: stash.__setitem__("cg", (v, r))))
            tasks.append(((win[:, :, 4096 + c * 128:4096 + (c + 1) * 128], [128, KC, 128]),
                          lambda v, r, c=c: m_compute(c, stash["cg"][0], stash["cg"][1], v, r)))
            tasks.append(((win[:, :, c * 128:(c + 1) * 128], [128, KC, 128]), lambda v, r, c=c: gate(c, v, r)))
            if c % 2 == 1:
                cp = c // 2
                for half in range(2):
                    tasks.append(((wout[:, 2 * cp:2 * cp + 2, half * 1024:(half + 1) * 1024], [128, 2, 1024]),
                                  lambda v, r, half=half: outp(half, v, r)))
        self.run_tasks(tasks, hold=1)
        P.dma("sp", Dm["halo_o"].rearrange("p (c m) -> p c m", m=2), mh, slot="hs", reads=["small"],
              writes=["dram_halo"])
        if 2 not in self.segs:
            P.dma("sp", Dm["bgh"], bgh, slot="hs", reads=["small"])

    def stage_conv_fix(self, l):
        P, I, Dm = self.P, self.I, self.Dm
        wout = I["b_w_out"].rearrange("(c p) n -> p c n", p=128)
        cw = self.small[:, 64:112].rearrange("p (w c) -> p w c", w=3)
        bgh = self.small[:, 0:32].rearrange("p (c m) -> p c m", m=2)
        lr = self.small[:, 32:64].rearrange("p (c m) -> p c m", m=2)
        if 1 not in self.segs:
            P.dma("sp", cw, I["b_convT"], slot="c2", reads=["small"], writes=["small"])
            P.dma("sp", bgh, Dm["bgh"], slot="c2", reads=["small"], writes=["small"])
        if "halo_lr" in I:
            P.dma("sp", lr, I["halo_lr"], slot="c2", reads=["small"], writes=["small"])
        else:
            hm = self.small[:, 112:128].rearrange("p (m r) -> p m r", m=2)
            P.dma("sp", hm, I["hmask"], slot="c2", reads=["small"], writes=["small"])
            ga = self.tab[:, 0, 0:256].rearrange("p (r c m) -> p r c m", r=8, m=2)
            P.dma("sp", ga, Dm["halo_a"].rearrange("(r p) (c m) -> p r c m", p=128, m=2), slot="c2",
                  reads=["dram_all_halo", "tab"], writes=["tab"] + self.TP)
            for side, mi in ((0, 1), (1, 0)):
                for r in range(8):
                    if r == 0:
                        P.op("dve", lambda e, side=side, mi=mi, r=r: e.tensor_scalar(
                            out=lr[:, :, side], in0=ga[:, r, :, mi], scalar1=hm[:, side, r:r + 1], scalar2=None,
                            op0=ALU.mult), reads=["tab", "small"], writes=["small"])
                    else:
                        P.op("dve", lambda e, side=side, mi=mi, r=r: e.scalar_tensor_tensor(
                            out=lr[:, :, side], in0=ga[:, r, :, mi], scalar=hm[:, side, r:r + 1], in1=lr[:, :, side],
                            op0=ALU.mult, op1=ALU.add), reads=["tab", "small"], writes=["small"])
        dg = self.sqb[0][:, 0:32].rearrange("p (c m) -> p c m", m=2)
        tf = self.small[:, 128:160].rearrange("p (c m) -> p c m", m=2)
        for side, wi in ((0, 0), (1, 2)):
            P.op("dve", lambda e, side=side, wi=wi: e.tensor_tensor(out=tf[:, :, side], in0=bgh[:, :, side],
                                                                   in1=cw[:, wi, :], op=ALU.mult),
                 reads=["small"], writes=["small"])
            P.op("dve", lambda e, side=side: e.tensor_tensor(out=dg[:, :, side], in0=tf[:, :, side],
                                                            in1=lr[:, :, side], op=ALU.mult),
                 reads=["small"], writes=["sq0"])
        ps, psn = self.bank()
        tasks = []
        for op8 in range(8):
            def comp(view, wres, op8=op8):
                for oi in range(2):
                    oc = 2 * op8 + oi
                    for k in range(KC):
                        P.op("pe", lambda e, k=k, oi=oi, oc=oc: e.matmul(
                            ps[:, 2 * oc:2 * oc + 2], view[:, k, oi * 128:(oi + 1) * 128], dg[:, k, :],
                            start=(k == 0), stop=(k == KC - 1)), reads=[wres, "sq0"], writes=[psn])
            tasks.append(((wout[:, :, op8 * 256:(op8 + 1) * 256], [128, KC, 256]), comp))
        self.run_tasks(tasks)
        pv = ps[:, 0:32].rearrange("p (c m) -> p c m", m=2)
        for side, col in ((0, 0), (1, 1023)):
            P.op("dve", lambda e, side=side: e.tensor_tensor(out=tf[:, :, side], in0=pv[:, :, side],
                                                            in1=self.mods[:, l, 32:48, 0], op=ALU.mult),
                 reads=[psn, "mods", "small"], writes=["small"])
            t = 0 if col == 0 else 1
            P.op("dve", lambda e, side=side, col=col: e.tensor_tensor(
                out=self.x[:, :, col], in0=self.x[:, :, col], in1=tf[:, :, side], op=ALU.add),
                reads=["small"] + ["x%d.%d" % (c, t) for c in range(KC)],
                writes=["x%d.%d" % (c, t) for c in range(KC)])

    def stage_mla_proj(self, l, L, exch=None):
        P, I, Dm = self.P, self.I, self.Dm
        self.stage_tables(64)
        kvo, kvc, qs = Dm["kvo%d" % L], Dm["kvc%d" % L], Dm["qs%d" % L]
        cn = self.small[:, 128:136].rearrange("p (a c) -> p a c", a=2)
        P.dma("sp", cn, I["c_nT"], slot="c2", reads=["small"], writes=["small"])
        A4 = self.aux[:, :].rearrange("p (c t) -> p c t", c=8)
        cq, ckv = A4[:, 0:4, :], A4[:, 4:8, :]
        wdq = I["c_w_dq"].rearrange("(c p) n -> p c n", p=128)
        wdkv = I["c_w_dkv"].rearrange("(c p) n -> p c n", p=128)
        tasks = []
        qtasks = []
        pn = {}

        def down(view, wres, dstb, pp, which, key):
            for ci in range(2):
                ch = 2 * pp + ci
                for t in range(3):
                    t0, tw = TILES[t]
                    ps, psn = self.psb[3 + (self.rot("dn", 2))], None
                    bi = 3 + ((self.slot_n["dn"] - 1) % 2)
                    psn = "ps%d" % bi
                    for k in range(KC):
                        P.op("pe", lambda e, k=k, ps=ps, ci=ci: e.matmul(
                            ps[:, :tw], view[:, k, ci * 128:(ci + 1) * 128], self.hv[:, k, t0:t0 + tw],
                            start=(k == 0), stop=(k == KC - 1)), reads=[wres, "h%d.%d" % (k, t)], writes=[psn])
                    si = self.rot("sq", 2)
                    sq = self.sqb[si]
                    P.op("act", lambda e, ps=ps, sq=sq: e.activation(out=sq[:, :tw], in_=ps[:, :tw], func=AF.Square),
                         reads=[psn], writes=["sq%d" % si])
                    P.op("act", lambda e, ps=ps, ch=ch: e.activation(out=dstb[:, ch, t0:t0 + tw], in_=ps[:, :tw],
                                                                    func=AF.Copy), reads=[psn], writes=[key])
                    acc = self.psb[t]
                    P.op("pe", lambda e, sq=sq, acc=acc, ch=ch: e.matmul(acc[:, :tw], self.ones, sq[:, :tw],
                                                                        start=(ch == 0), stop=(ch == 3)),
                         reads=["sq%d" % si, "cm"], writes=["ps%d" % t])

        def down_fin(dstb, which, key):
            for t in range(3):
                t0, tw = TILES[t]
                rs = self.rstd[:, t0:t0 + tw]
                acc = self.psb[t]
                P.op("dve", lambda e, acc=acc, rs=rs: e.tensor_scalar(out=rs, in0=acc[:, :tw], scalar1=1.0 / 512,
                                                                     scalar2=EPS, op0=ALU.mult, op1=ALU.add),
                     reads=["ps%d" % t], writes=["rstd%d" % t])
                self.rsqrt_inplace(rs, "rstd%d" % t)
                for ch in range(4):
                    P.op("dve", lambda e, ch=ch, rs=rs: e.scalar_tensor_tensor(
                        out=dstb[:, ch, t0:t0 + tw], in0=dstb[:, ch, t0:t0 + tw], scalar=cn[:, which, ch:ch + 1],
                        in1=rs, op0=ALU.mult, op1=ALU.mult), reads=[key, "rstd%d" % t, "small"], writes=[key])

        for pp in range(2):
            tasks.append(((wdq[:, :, pp * 256:(pp + 1) * 256], [128, KC, 256]),
                          lambda v, r, pp=pp: down(v, r, cq, pp, 0, "aq")))
        tasks.append((None, lambda: down_fin(cq, 0, "aq")))
        for pp in range(2):
            tasks.append(((wdkv[:, :, pp * 256:(pp + 1) * 256], [128, KC, 256]),
                          lambda v, r, pp=pp: down(v, r, ckv, pp, 1, "ao")))
        tasks.append((None, lambda: down_fin(ckv, 1, "ao")))

        def krope(view, wres):
            for t in range(3):
                t0, tw = TILES[t]
                ps, psn = self.bank()
                for k in range(KC):
                    P.op("pe", lambda e, k=k, ps=ps: e.matmul(ps[0:64, :tw], view[:, k, :], self.hv[:, k, t0:t0 + tw],
                                                             start=(k == 0), stop=(k == KC - 1)),
                         reads=[wres, "h%d.%d" % (k, t)], writes=[psn])
                dst = kvc[4096:4160, :] if t == 2 else kvo[4096:4160, t0:t0 + tw]
                self.head_post(ps, psn, t, 64, None, None, self.rot64, dst)
        tasks.append(((wdkv[:, :, 512:576], [128, KC, 64]), krope))

        wuq = I["c_w_uq"].rearrange("(c p) (h e) -> p c h e", p=128, e=192)
        for hp in range(2):
            def qn(view, wres, hp=hp):
                for hi in range(8):
                    hd = 8 * hp + hi
                    for t in range(3):
                        t0, tw = TILES[t]
                        ps, psn = self.bank()
                        for k in range(4):
                            P.op("pe", lambda e, k=k, ps=ps, hi=hi: e.matmul(
                                ps[:, :tw], view[:, k, hi, :], cq[:, k, t0:t0 + tw], start=(k == 0), stop=(k == 3)),
                                reads=[wres, "aq"], writes=[psn])
                        self.head_post(ps, psn, t, 128, None, None, None, qs[hd * 128:(hd + 1) * 128, t0:t0 + tw],
                                       rope=False, dres="dram_q")
            qtasks.append(((wuq[:, :, 8 * hp:8 * hp + 8, 0:128], [128, 4, 8, 128]), qn))

        def qr(view, wres):
            view = view.rearrange("p c h e -> p c (h e)")
            for hp2 in range(8):
                for t in range(3):
                    t0, tw = TILES[t]
                    ps, psn = self.bank()
                    for k in range(4):
                        P.op("pe", lambda e, k=k, ps=ps, hp2=hp2: e.matmul(
                            ps[:, :tw], view[:, k, hp2 * 128:(hp2 + 1) * 128], cq[:, k, t0:t0 + tw],
                            start=(k == 0), stop=(k == 3)), reads=[wres, "aq"], writes=[psn])
                    self.head_post(ps, psn, t, 128, None, None, self.rot64,
                                   qs[2048 + hp2 * 128:2048 + (hp2 + 1) * 128, t0:t0 + tw], dres="dram_q")
        qtasks.append(((wuq[:, :, :, 128:192], [128, 4, 16, 64]), qr))

        wukv = I["c_w_ukv"].rearrange("(c p) (h t d) -> p c h t d", p=128, t=2, d=128)
        for hp in range(2):
            def kn(view, wres, hp=hp):
                for hi in range(8):
                    hd = 8 * hp + hi
                    for t in range(3):
                        t0, tw = TILES[t]
                        ps, psn = self.bank()
                        for k in range(4):
                            P.op("pe", lambda e, k=k, ps=ps, hi=hi: e.matmul(
                                ps[:, :tw], view[:, k, hi, :], ckv[:, k, t0:t0 + tw], start=(k == 0), stop=(k == 3)),
                                reads=[wres, "ao"], writes=[psn])
                        dst = kvc[hd * 128:(hd + 1) * 128, :] if t == 2 else kvo[hd * 128:(hd + 1) * 128, t0:t0 + tw]
                        self.head_post(ps, psn, t, 128, None, None, None, dst, rope=False)
            tasks.append(((wukv[:, :, 8 * hp:8 * hp + 8, 0, :], [128, 4, 8, 128]), kn))
        for hq in range(4):
            def vdst(jt, hq=hq):
                r0 = 2048 + hq * 512
                if jt < 8:
                    return kvo[r0:r0 + 512, jt * 128:(jt + 1) * 128].rearrange("(g p) d -> p g d", p=128)
                return kvc[r0:r0 + 512, (jt - 8) * 128:(jt - 7) * 128].rearrange("(g p) d -> p g d", p=128)

            def vv(view, wres, vdst=vdst):
                v3 = view.rearrange("p c h d -> p c (h d)")
                self.v_tiles(v3, wres, lambda k, jt: ckv[:, k, jt * 128:(jt + 1) * 128], lambda k, t: "ao", 4, 512, vdst)
            tasks.append(((wukv[:, :, 4 * hq:4 * hq + 4, 1, :], [128, 4, 4, 128]), vv))
        if exch is not None:
            tasks.append((None, exch))
        tasks += qtasks
        self.run_tasks(tasks)

    def stage_mla_attn(self, l, L):
        P, I, Dm = self.P, self.I, self.Dm
        kva, kvc, qs = Dm["kva%d" % L], Dm["kvc%d" % L], Dm["qs%d" % L]
        Kb = self.h[:, 0:NKEY]
        Vb = self.h[:, NKEY:2 * NKEY].rearrange("p (c d) -> p c d", d=128)
        A4 = self.aux[:, :].rearrange("p (c t) -> p c t", c=8)
        Og = A4[:, 4:8, :]
        kva3 = kva.rearrange("(r n) t -> n r t", r=8)
        krb = self.wring[3]
        self.ring, self.wl_n = 3, 0
        scale = 192 ** -0.5
        hres = ["h%d.%d" % (c, t) for c in range(KC) for t in range(3)]
        P.dma("sp", krb[0:64, 0:4096].rearrange("p (r t) -> p r t", r=4), kva3[4096:4160, 0:4, :], slot="kr",
              reads=["dram_all_kv%d" % L], writes=["w3"])
        P.dma("sp", krb[0:64, 4096:4224], kva3[4096:4160, 4, 0:128], slot="kr", reads=["dram_all_kv%d" % L],
              writes=["w3"])
        P.dma("sp", krb[64:128, 0:896], kva3[4096:4160, 4, 128:1024], slot="kr", reads=["dram_all_kv%d" % L],
              writes=["w3"])
        P.dma("sp", krb[64:128, 896:3968].rearrange("p (r t) -> p r t", r=3), kva3[4096:4160, 5:8, :], slot="kr",
              reads=["dram_all_kv%d" % L], writes=["w3"])
        P.dma("sp", krb[64:128, 3968:4224], kvc[4096:4160, :], slot="kr", reads=["dram_kv"], writes=["w3"])
        for hd in range(16):
            g, hh = hd // 4, hd % 4
            self.kv_load(Kb, Vb, kva3, kvc, hd * 128, 2048 + hd * 128, L)
            qi = hd % 2
            Qn, Qr = A4[:, 2 * qi, :], A4[:, 2 * qi + 1, :]
            qres = "aq%d" % qi
            P.dma("sp", Qn, qs[hd * 128:(hd + 1) * 128, :], slot="q%d" % qi, reads=["dram_q"], writes=[qres, "aq"])
            P.dma("sp", Qr[0:64, :], qs[2048 + hd * 64:2048 + (hd + 1) * 64, :], slot="q%d" % qi, reads=["dram_q"],
                  writes=[qres])
            P.dma("sp", Qr[64:128, :], qs[2048 + hd * 64:2048 + (hd + 1) * 64, :], slot="q%d" % qi,
                  reads=["dram_q"], writes=[qres])
            for t in range(3):
                t0, tw = TILES[t]
                chunks = list(range(NCH)) if t != 2 else [64, 65]

                def kq(ck, t0=t0, tw=tw, Qn=Qn, Qr=Qr):
                    if ck < 33:
                        kr = (krb[0:64, ck * 128:(ck + 1) * 128], Qr[0:64, t0:t0 + tw])
                    else:
                        kr = (krb[64:128, (ck - 33) * 128:(ck - 32) * 128], Qr[64:128, t0:t0 + tw])
                    return [(Kb[:, ck * 128:(ck + 1) * 128], Qn[:, t0:t0 + tw]), kr]
                self.attn_unit(t0, tw, chunks, kq, lambda ck: Vb[:, ck, :], scale, Og[:, hh, t0:t0 + tw],
                               ["w3", qres], "ao")
            if hh == 3:
                self.run_tasks(self.wo_tasks(l, I["c_w_o"], g, [0, 1, 2], Og))
        self.ring, self.wl_n = NSLOT, 0

    def seg0(self):
        self.layer_vecs(0)
        self.stage_norm(0, 0, [0, 1, 2])
        ex = (lambda: self.exchange(self.Dm["kvo0"], self.Dm["kva0"], "kv0")) if 1 in self.segs else None
        self.stage_gqa_proj(0, 0, 0, True, ex)

    def seg1(self):
        if 0 not in self.segs:
            self.layer_vecs(0)
        if "noattn" not in DBG:
            self.stage_gqa_attn(0, 0, 0, True)
        if "nomlp" not in DBG:
            self.stage_mlp(0, [0, 1, 2])
        self.layer_vecs(1)
        if "noconv" not in DBG:
            self.stage_norm(1, 0, [0, 1, 2])
            self.stage_conv_main(1)
        if 2 in self.segs:
            self.exchange(self.Dm["halo_o"], self.Dm["halo_a"], "halo")

    def seg2(self):
        if 1 not in self.segs:
            self.layer_vecs(1)
        self.stage_conv_fix(1)
        self.stage_mlp(1, [0, 1, 2])
        self.layer_vecs(2)
        self.stage_norm(2, 0, [0, 1, 2])
        ex = (lambda: self.exchange(self.Dm["kvo2"], self.Dm["kva2"], "kv2")) if 3 in self.segs else None
        self.stage_mla_proj(2, 2, ex)

    def seg3(self):
        if 2 not in self.segs:
            self.layer_vecs(2)
        self.stage_mla_attn(2, 2)
        self.stage_mlp(2, [0, 1, 2])
        self.layer_vecs(3)
        self.stage_norm(3, 0, [0, 1, 2])
        ex = (lambda: self.exchange(self.Dm["kvo3"], self.Dm["kva3"], "kv3")) if 4 in self.segs else None
        self.stage_gqa_proj(3, 1, 3, False, ex)

    def seg4(self):
        if 3 not in self.segs:
            self.layer_vecs(3)
        self.stage_gqa_attn(3, 1, 3, False)
        self.stage_mlp(3, [0, 1])
        fn = self.small[:, 136:152]
        self.P.dma("sp", fn, self.I["fnT"], slot="c2", reads=["small"], writes=["fn"])
        self.I_fn = fn
        outT = self.dout("outT", [128, KC, NL])
        self.stage_norm(3, 0, [0, 1], final=True, outT=outT)


def _fm(v):
    v = np.asarray(v, dtype=np.float32)
    lead = v.shape[:-1]
    n = v.shape[-1] // 128
    return np.ascontiguousarray(np.moveaxis(v.reshape(*lead, n, 128), -1, 0))


def _consts():
    cm = np.zeros((128, 3, 128), np.float32)
    cm[:, 0, :] = 1.0
    k = np.arange(128)
    cm[k, 1, (k + 64) % 128] = 1.0
    cm[k, 2, k ^ 32] = 1.0
    return cm


def _rope_tables(r):
    tok = r * NL + np.arange(NL)
    row, col = (tok // 64).astype(np.float32), (tok % 64).astype(np.float32)
    k = np.arange(128)
    out = np.zeros((2, 128, 2, NL), np.float32)
    for bi, blk in enumerate((128, 64)):
        pp = k % blk
        half, quarter = blk // 2, blk // 4
        jj = pp % half
        inv = (np.float32(10000.0) ** (-(jj % quarter).astype(np.float32) / np.float32(quarter))).astype(np.float32)
        pos = np.where((jj < quarter)[:, None], row[None, :], col[None, :]).astype(np.float32)
        ang = (pos * inv[:, None]).astype(np.float32)
        sgn = np.where(pp < half, -1.0, 1.0).astype(np.float32)
        out[bi, :, 0, :] = np.cos(ang)
        out[bi, :, 1, :] = np.sin(ang) * sgn[:, None]
    return out


def _prepare(x, c, ctx, c_ctx, w_ada, b_ada, norm1, norm2, w_mlp1, w_mlp2,
             a_w_qkv, a_q_norm, a_k_norm, a_w_o, b_w_in, b_conv, b_w_out,
             c_w_dq, c_q_norm, c_w_uq, c_w_dkv, c_kv_norm, c_w_ukv, c_w_o, final_norm):
    f = lambda a: np.ascontiguousarray(np.asarray(a, dtype=np.float32))
    cm = _consts()
    x2, ctx2 = f(x)[0], f(ctx)[0]
    shared = {
        "cmat": cm,
        "n1T": np.ascontiguousarray(_fm(norm1)), "n2T": np.ascontiguousarray(_fm(norm2)),
        "w_mlp1": f(w_mlp1), "w_mlp2": f(w_mlp2),
        "cvec": np.ascontiguousarray(np.stack([_fm(f(c)[0]), _fm(c_ctx)], axis=-1)),
        "w_ada": f(w_ada), "b_adaT": np.ascontiguousarray(_fm(b_ada)),
        "a_w_qkv": f(a_w_qkv), "a_w_o": f(a_w_o),
        "a_qk": np.ascontiguousarray(np.stack([f(a_q_norm).T, f(a_k_norm).T], axis=-1)),
        "b_w_in": f(b_w_in)[0], "b_convT": np.ascontiguousarray(_fm(f(b_conv)[0])), "b_w_out": f(b_w_out)[0],
        "c_w_dq": f(c_w_dq)[0], "c_w_uq": f(c_w_uq)[0], "c_w_dkv": f(c_w_dkv)[0], "c_w_ukv": f(c_w_ukv)[0],
        "c_w_o": f(c_w_o)[0],
        "c_nT": np.ascontiguousarray(np.stack([_fm(f(c_q_norm)[0]), _fm(f(c_kv_norm)[0])], axis=1)),
        "fnT": np.ascontiguousarray(_fm(final_norm)),
    }
    per = []
    for r in range(NCORES):
        d = {}
        d["xT"] = np.ascontiguousarray(np.concatenate([x2[r * NL:(r + 1) * NL].T, ctx2.T], axis=1))
        d["ropetab"] = _rope_tables(r)
        hm = np.zeros((128, 2, 8), np.float32)
        if r > 0:
            hm[:, 0, r - 1] = 1.0
        if r < 7:
            hm[:, 1, r + 1] = 1.0
        d["hmask"] = hm
        d["w_ada_sh"] = (f(w_ada), r)
        d["b_adaT_sh"] = np.ascontiguousarray(shared["b_adaT"][:, :, r * 12:(r + 1) * 12])
        per.append(d)
    return shared, per


def _launch(segs, shared, per, state, cores=None):
    cores = list(range(NCORES)) if cores is None else cores
    b = Builder(segs)
    nc = b.build()
    in_maps = []
    for r in cores:
        m = {}
        for name in b.ext_in:
            if name in ("w_mlp1", "w_mlp2"):
                m[name] = np.ascontiguousarray(shared[name][b.mlp_layers])
            elif name == "w_ada_sh":
                wa, rr = per[r][name]
                m[name] = np.ascontiguousarray(wa[:, :, rr * 1536:(rr + 1) * 1536])
            elif name in per[r]:
                m[name] = per[r][name]
            elif name in shared:
                m[name] = shared[name]
            else:
                m[name] = state[r][name]
        in_maps.append(m)
    for name in ("w_mlp1", "w_mlp2"):
        if name in b.ext_in:
            for m in in_maps[1:]:
                m[name] = in_maps[0][name]
    res = run_bass_kernel_spmd(nc, in_maps, core_ids=list(range(len(cores))))
    outs = dict(zip(cores, res.results))
    for r in cores:
        o = outs[r]
        for name in b.ext_out:
            if name == "x_out":
                state[r]["x_in"] = o[name]
            elif name == "mods_out":
                state[r]["mods_in"] = o[name]
            else:
                state[r][name] = o[name]
    if len(cores) < NCORES:
        return state
    for name in b.ext_out:
        if name.startswith("kvo"):
            allv = np.concatenate([outs[r][name] for r in range(NCORES)], axis=0)
            for r in range(NCORES):
                state[r]["kva" + name[3:]] = allv
        if name == "halo_o":
            hs = [np.asarray(outs[r][name]).reshape(128, 16, 2) for r in range(NCORES)]
            for r in range(NCORES):
                lr = np.zeros((128, 16, 2), np.float32)
                if r > 0:
                    lr[:, :, 0] = hs[r - 1][:, :, 1]
                if r < 7:
                    lr[:, :, 1] = hs[r + 1][:, :, 0]
                state[r]["halo_lr"] = lr
    return state


def kernel(**inputs):
    shared, per = _prepare(**inputs)
    launches = [[0, 1, 2, 3, 4]] if FUSED else [[0], [1], [2], [3], [4]]
    state = [dict() for _ in range(NCORES)]
    for segs in launches:
        state = _launch(segs, shared, per, state)
    out = np.empty((1, NCORES * NL, D), np.float32)
    for r in range(NCORES):
        oT = np.asarray(state[r]["outT"])
        out[0, r * NL:(r + 1) * NL, :] = oT.transpose(2, 1, 0).reshape(NL, D)
    return out
```

```python
import math
from contextlib import ExitStack
import numpy as np
import concourse.bass as bass
import concourse.mybir as mybir
from concourse.bass_utils import run_bass_kernel_spmd

F32 = mybir.dt.float32
BF16 = mybir.dt.bfloat16
ALU = mybir.AluOpType
AF = mybir.ActivationFunctionType

FUSED = True
SAME_ENGINE_SYNC = True
NCORES = 8
D = 2048
KC = 16
NL = 1024
NCX = 256
NT = NL + NCX
TILES = [(0, 512), (512, 512), (1024, 256)]
EPS = 1e-6
NKEY = 8192 + 256
NCH = NKEY // 128
WSLOT = 4224
NSLOT = 4


class _Op:
    __slots__ = ("eng", "fn", "reads", "writes", "slot", "deps", "sig", "cnt", "idx", "ep")

    def __init__(self, eng, fn, reads, writes, slot):
        self.eng, self.fn, self.reads, self.writes, self.slot = eng, fn, reads, writes, slot
        self.deps = ()
        self.sig = False
        self.cnt = 0


class _Rec:
    def __init__(self):
        self.calls = []

    def __getattr__(self, name):
        def f(*args, **kwargs):
            self.calls.append((name, args, kwargs))
        return f


class Prog:
    ENGS = ("pe", "act", "dve", "pool", "sp")

    def __init__(self, nc):
        self.nc = nc
        self.ops = []
        self.stack = ExitStack()
        self.epoch = 0

    def sb(self, name, shape, dtype):
        return self.stack.enter_context(self.nc.sbuf_tensor("sb_" + name, list(shape), dtype))

    def ps(self, name, shape, dtype=F32):
        return self.stack.enter_context(self.nc.psum_tensor(name, list(shape), dtype))

    def op(self, eng, fn, reads=(), writes=(), slot=None):
        rec = _Rec()
        fn(rec)
        assert len(rec.calls) == 1
        o = _Op(eng, rec.calls[0], tuple(reads), tuple(writes), slot)
        o.idx = len(self.ops)
        o.ep = self.epoch
        self.ops.append(o)
        return o

    def dma(self, eng, out, in_, slot, reads=(), writes=(), **kw):
        return self.op(eng, lambda e: e.dma_start(out=out, in_=in_, **kw), reads, writes, slot)

    def build(self):
        nc, ops = self.nc, self.ops
        lw, rs = {}, {}
        slot_last = {}
        for o in ops:
            deps = set()
            if o.slot is not None:
                key = o.writes if o.writes else o.reads
                prev = slot_last.get(o.slot)
                if prev is not None and prev[0] != key:
                    deps.add(prev[1])
                slot_last[o.slot] = (key, o.idx)
            for r in o.reads:
                deps.update(lw.get(r, ()))
            for w in o.writes:
                deps.update(lw.get(w, ()))
                deps.update(rs.get(w, ()))
            for r in o.reads:
                rs.setdefault(r, []).append(o.idx)
            for w in o.writes:
                prev = lw.get(w, [])
                if o.slot is not None and prev and all(ops[j].slot is not None for j in prev):
                    keep = {}
                    for j in prev + [o.idx]:
                        keep[ops[j].slot] = j
                    lw[w] = sorted(keep.values())
                else:
                    lw[w] = [o.idx]
                rs[w] = []
            deps.discard(o.idx)
            o.deps = sorted(deps)
        for o in ops:
            for j in o.deps:
                d = ops[j]
                if d.slot is not None:
                    continue
                if d.eng != o.eng or o.slot is not None:
                    d.sig = True
                elif SAME_ENGINE_SYNC and d.eng != "pe":
                    d.sig = True
        cnt = {}
        slotcnt = {}
        for o in ops:
            if o.slot is not None:
                slotcnt[o.slot] = slotcnt.get(o.slot, 0) + 16
                o.cnt = slotcnt[o.slot]
            elif o.sig:
                cnt[(o.eng, o.ep)] = cnt.get((o.eng, o.ep), 0) + 1
                o.cnt = cnt[(o.eng, o.ep)]
        st = self.stack
        esem = {k: st.enter_context(nc.semaphore("s_%s_%d" % k)) for k in cnt}
        ssem = {k: st.enter_context(nc.semaphore("d_" + str(k))) for k in slotcnt}
        block = st.enter_context(nc.Block())

        def run_engine(ename, eng):
            seen = {}
            for o in ops:
                if o.eng != ename:
                    continue
                need = {}
                for j in o.deps:
                    d = ops[j]
                    if d.slot is not None:
                        key, sem = ("slot", d.slot), ssem[d.slot]
                    else:
                        if d.eng == ename and o.slot is None:
                            if not (SAME_ENGINE_SYNC and ename != "pe"):
                                continue
                        key, sem = ("eng", d.eng, d.ep), esem[(d.eng, d.ep)]
                    if need.get(key, (None, 0))[1] < d.cnt:
                        need[key] = (sem, d.cnt)
                for key, (sem, c) in need.items():
                    if seen.get(key, 0) >= c:
                        continue
                    seen[key] = c
                    eng.wait_ge(sem, c)
                name, args, kwargs = o.fn
                ins = getattr(eng, name)(*args, **kwargs)
                if o.slot is not None:
                    ins.then_inc(ssem[o.slot], 16)
                elif o.sig:
                    ins.then_inc(esem[(ename, o.ep)], 1)
            last = {}
            for o in ops:
                if o.eng == ename and o.slot is not None:
                    last[o.slot] = max(last.get(o.slot, 0), o.cnt)
            for k, v in last.items():
                if seen.get(("slot", k), 0) < v:
                    eng.wait_ge(ssem[k], v)

        block.tensor(lambda e: run_engine("pe", e))
        block.scalar(lambda e: run_engine("act", e))
        block.vector(lambda e: run_engine("dve", e))
        block.gpsimd(lambda e: run_engine("pool", e))
        block.sync(lambda e: run_engine("sp", e))

    def close(self):
        self.stack.close()


DBG = set()
LAYERS = [("gqa", 0), ("conv", 0), ("mla", 0), ("gqa", 1)]


class Builder:
    def __init__(self, segs):
        self.segs = list(segs)
        self.nc = bass.Bass("TRN2", target_bir_lowering=False)
        self.P = Prog(self.nc)
        self.ext_in = []
        self.ext_out = []
        self.bank_i = 0
        self.wl_n = 0
        self.tmp_i = 0
        self.slot_n = {}
        self.ring = NSLOT

    def din(self, name, shape, dt=F32):
        self.ext_in.append(name)
        return self.nc.dram_tensor(name, list(shape), dt, kind="ExternalInput").ap()

    def dout(self, name, shape, dt=F32):
        self.ext_out.append(name)
        return self.nc.dram_tensor(name, list(shape), dt, kind="ExternalOutput").ap()

    def dstate(self, name, shape, dt, prod, cons):
        p, c = prod in self.segs, cons in self.segs
        if p and c:
            return self.nc.dram_tensor(name, list(shape), dt).ap()
        if p:
            return self.dout(name, shape, dt)
        if c:
            return self.din(name, shape, dt)
        return None

    def bank(self):
        b = self.bank_i % 8
        self.bank_i += 1
        return self.psb[b], "ps%d" % b

    def tmp(self):
        i = self.tmp_i % 3
        self.tmp_i += 1
        return self.tmpb[i], "tmp%d" % i

    def rot(self, key, n):
        i = self.slot_n.get(key, 0)
        self.slot_n[key] = i + 1
        return i % n

    def wload(self, src, shape):
        s = self.wl_n % self.ring
        self.wl_n += 1
        n = 1
        for d in shape[1:]:
            n *= d
        view = self.wring[s][:, 0:n]
        if len(shape) == 3:
            view = view.rearrange("p (a b) -> p a b", a=shape[1])
        elif len(shape) == 4:
            view = view.rearrange("p (a b c) -> p a b c", a=shape[1], b=shape[2])
        if len(shape) == 4:
            for a in range(shape[1]):
                self.P.dma("pool", view[:, a], src[:, a], slot="w%d" % s, writes=["w%d" % s])
        else:
            self.P.dma("pool", view, src, slot="w%d" % s, writes=["w%d" % s])
        return view, "w%d" % s

    def run_tasks(self, tasks, hold=0):
        lidx = [i for i, t in enumerate(tasks) if t[0] is not None]
        views = {}
        li = 0
        done = 0
        for i, (ls, comp) in enumerate(tasks):
            while li < len(lidx) and li < done + self.ring - hold:
                src, shape = tasks[lidx[li]][0]
                views[lidx[li]] = self.wload(src, shape)
                li += 1
            if ls is not None:
                comp(*views.pop(i))
                done += 1
            else:
                comp()

    def build(self):
        P, nc, segs = self.P, self.nc, self.segs
        first, last = segs[0], segs[-1]
        I = {}
        I["cmat"] = self.din("cmat", [128, 3, 128])
        I["ropetab"] = self.din("ropetab", [2, 128, 2, NL])
        I["n1T"] = self.din("n1T", [128, 4, 16])
        I["n2T"] = self.din("n2T", [128, 4, 16])
        self.mlp_layers = [l for l, sg in ((0, 1), (1, 2), (2, 3), (3, 4)) if sg in segs]
        if self.mlp_layers:
            I["w_mlp1"] = self.din("w_mlp1", [len(self.mlp_layers), D, 4 * D])
            I["w_mlp2"] = self.din("w_mlp2", [len(self.mlp_layers), 4 * D, D])
        if 0 in segs:
            I["xT"] = self.din("xT", [D, NT])
            I["cvec"] = self.din("cvec", [128, 16, 2])
            self.ada_shard = (segs == [0, 1, 2, 3, 4])
            if self.ada_shard:
                I["w_ada_sh"] = self.din("w_ada_sh", [4, D, 1536])
                I["b_adaT_sh"] = self.din("b_adaT_sh", [128, 4, 12])
            else:
                I["w_ada"] = self.din("w_ada", [4, D, 6 * D])
                I["b_adaT"] = self.din("b_adaT", [128, 4, 96])
        if first > 0:
            I["x_in"] = self.din("x_in", [128, KC, NT])
            I["mods_in"] = self.din("mods_in", [128, 4, 96, 2])
        if any(s in segs for s in (0, 1, 3, 4)):
            I["a_w_qkv"] = self.din("a_w_qkv", [2, D, 3072])
            I["a_w_o"] = self.din("a_w_o", [2, D, D])
            I["a_qk"] = self.din("a_qk", [128, 2, 2])
        if any(s in segs for s in (1, 2)):
            I["b_w_in"] = self.din("b_w_in", [D, 3 * D])
            I["b_convT"] = self.din("b_convT", [128, 3, 16])
            I["b_w_out"] = self.din("b_w_out", [D, D])
        if any(s in segs for s in (2, 3)):
            I["c_w_dq"] = self.din("c_w_dq", [D, 512])
            I["c_w_uq"] = self.din("c_w_uq", [512, 3072])
            I["c_w_dkv"] = self.din("c_w_dkv", [D, 576])
            I["c_w_ukv"] = self.din("c_w_ukv", [512, 4096])
            I["c_w_o"] = self.din("c_w_o", [D, D])
            I["c_nT"] = self.din("c_nT", [128, 2, 4])
        if 4 in segs:
            I["fnT"] = self.din("fnT", [128, 16])
        if 2 in segs:
            if 1 in segs:
                I["hmask"] = self.din("hmask", [128, 2, 8])
            else:
                I["halo_lr"] = self.din("halo_lr", [128, 16, 2])
        self.I = I
        Dm = {}
        kvrows = {0: 1024, 2: 4160, 3: 1024}
        qrows = {0: 16 * 128, 2: 16 * 128 + 16 * 64, 3: 16 * 128}
        for L, sp in ((0, 0), (2, 2), (3, 3)):
            Dm["qs%d" % L] = self.dstate("qs%d" % L, [qrows[L], NT], BF16, sp, sp + 1)
            Dm["kvc%d" % L] = self.dstate("kvc%d" % L, [kvrows[L], NCX], BF16, sp, sp + 1)
            if sp in segs and sp + 1 in segs:
                Dm["kvo%d" % L] = nc.dram_tensor("kvo%d" % L, [kvrows[L], NL], BF16).ap()
                Dm["kva%d" % L] = nc.dram_tensor("kva%d" % L, [8 * kvrows[L], NL], BF16).ap()
            elif sp in segs:
                Dm["kvo%d" % L] = self.dout("kvo%d" % L, [kvrows[L], NL], BF16)
            elif sp + 1 in segs:
                Dm["kva%d" % L] = self.din("kva%d" % L, [8 * kvrows[L], NL], BF16)
        if 1 in segs and 2 in segs:
            Dm["halo_o"] = nc.dram_tensor("halo_o", [128, 32], F32).ap()
            Dm["halo_a"] = nc.dram_tensor("halo_a", [8 * 128, 32], F32).ap()
        elif 1 in segs:
            Dm["halo_o"] = self.dout("halo_o", [128, 32], F32)
        Dm["bgh"] = self.dstate("bgh", [128, 16, 2], F32, 1, 2)
        self.Dm = Dm
        self.x = P.sb("x", [128, KC, NT], F32)
        self.h = P.sb("h", [128, KC * NT], BF16)
        self.aux = P.sb("aux", [128, 8 * NT], BF16)
        self.wring = [P.sb("wr%d" % i, [128, WSLOT if i == 3 else 4096], BF16) for i in range(NSLOT)]
        self.tab = P.sb("tab", [128, 2, NL], F32)
        self.ptb = [P.sb("pt%d" % i, [128, 512], BF16) for i in range(4)]
        self.tmpb = [P.sb("tmp%d" % i, [128, 512], F32) for i in range(3)]
        self.sqb = [P.sb("sq%d" % i, [128, 512], BF16) for i in range(2)]
        self.stg = [P.sb("stg%d" % i, [128, 512], BF16) for i in range(2)]
        self.rstd = P.sb("rstd", [128, NT], F32)
        self.mods = P.sb("mods", [128, 4, 96, 2], F32)
        self.vecA = P.sb("vecA", [128, 2, 16, 2], F32)
        self.cm = P.sb("cm", [128, 3, 128], BF16)
        self.nT = P.sb("nT", [128, 2, 4, 16], F32)
        self.small = P.sb("small", [128, 160], F32)
        self.silb = P.sb("silb", [128, 32], BF16)
        self.psb = [P.ps("psb%d" % i, [128, 512]) for i in range(8)]
        self.mbuf = self.tab[:, :, :].rearrange("p a t -> p (a t)")[:, 0:1288]
        tb16 = self.tab[:, :, :].rearrange("p a t -> p (a t)").bitcast(BF16)
        self.tpb = [tb16[:, 512 * i:512 * (i + 1)] for i in range(8)]
        self.TP = ["tp%d" % i for i in range(8)]
        self.ones = self.cm[:, 0, :]
        self.rot128 = self.cm[:, 1, :]
        self.rot64 = self.cm[:, 2, :]
        hv = self.h[:, :].rearrange("p (c t) -> p c t", c=KC)
        self.hv = hv
        P.dma("pool", self.cm[:], I["cmat"], slot="c0", writes=["cm"])
        P.dma("sp", self.nT[:, 0], I["n1T"], slot="c1", writes=["nT"])
        P.dma("sp", self.nT[:, 1], I["n2T"], slot="c1", writes=["nT"])
        if first == 0:
            xv = I["xT"].rearrange("(c p) t -> p c t", p=128)
            for c4 in range(4):
                P.dma("sp", self.x[:, 4 * c4:4 * c4 + 4, :], xv[:, 4 * c4:4 * c4 + 4, :], slot="xl",
                      writes=["x%d.%d" % (c, t) for c in range(4 * c4, 4 * c4 + 4) for t in range(3)])
            if self.ada_shard:
                self.stage_ada_sharded()
            else:
                self.stage_ada()
        else:
            for c4 in range(4):
                P.dma("sp", self.x[:, 4 * c4:4 * c4 + 4, :], I["x_in"][:, 4 * c4:4 * c4 + 4, :], slot="xl",
                      writes=["x%d.%d" % (c, t) for c in range(4 * c4, 4 * c4 + 4) for t in range(3)])
            P.dma("sp", self.mods[:], I["mods_in"], slot="c1", writes=["mods"])
        for s in segs:
            P.epoch = s
            getattr(self, "seg%d" % s)()
            if "dumpx" in DBG and s < 4:
                xd = self.dout("xdbg%d" % s, [128, KC, NT])
                for c4 in range(4):
                    P.dma("sp", xd[:, 4 * c4:4 * c4 + 4, :], self.x[:, 4 * c4:4 * c4 + 4, :], slot="xs",
                          reads=["x%d.%d" % (c, t) for c in range(4 * c4, 4 * c4 + 4) for t in range(3)])
                if s == 0:
                    md = self.dout("modsdbg", [128, 4, 96, 2])
                    P.dma("sp", md, self.mods[:], slot="xs", reads=["mods"])
        if last < 4:
            xo = self.dout("x_out", [128, KC, NT])
            mo = self.dout("mods_out", [128, 4, 96, 2])
            for c4 in range(4):
                P.dma("sp", xo[:, 4 * c4:4 * c4 + 4, :], self.x[:, 4 * c4:4 * c4 + 4, :], slot="xs",
                      reads=["x%d.%d" % (c, t) for c in range(4 * c4, 4 * c4 + 4) for t in range(3)])
            P.dma("sp", mo, self.mods[:], slot="xs", reads=["mods"])
        P.build()
        P.close()
        return nc

    def stage_ada(self):
        P, I = self.P, self.I
        cv = self.small[:, 0:32].rearrange("p (k m) -> p k m", m=2)
        sil = self.silb[:, :].rearrange("p (k m) -> p k m", m=2)
        bt = self.small[:, 32:128]
        P.dma("sp", cv, I["cvec"], slot="c1", writes=["small"])
        P.op("act", lambda e: e.activation(out=sil, in_=cv, func=AF.Silu), reads=["small"], writes=["sil"])
        tasks = []
        for l in range(4):
            wv = I["w_ada"][l].rearrange("(c p) n -> p c n", p=128)
            ps, psn = self.bank()

            def comp(view, wres, l=l, ps=ps, psn=psn, j0=0):
                for jj in range(2):
                    j = j0 + jj
                    for k in range(KC):
                        P.op("pe", lambda e, k=k, jj=jj, j=j: e.matmul(
                            ps[:, 2 * j:2 * j + 2], view[:, k, jj * 128:(jj + 1) * 128], sil[:, k, :],
                            start=(k == 0), stop=(k == KC - 1)), reads=[wres, "sil"], writes=[psn])

            for jp in range(48):
                tasks.append(((wv[:, :, jp * 256:(jp + 1) * 256], [128, KC, 256]),
                              lambda v, r, comp=comp, jp=jp: comp(v, r, j0=2 * jp)))

            def fin(l=l, ps=ps, psn=psn):
                P.dma("sp", bt, I["b_adaT"][:, l, :], slot="c2", writes=["bt"])
                pv = ps[:, 0:192].rearrange("p (j m) -> p j m", m=2)
                for m in range(2):
                    P.op("dve", lambda e, m=m: e.tensor_tensor(out=self.mods[:, l, :, m], in0=pv[:, :, m], in1=bt,
                                                              op=ALU.add), reads=[psn, "bt"], writes=["mods"])
            tasks.append((None, fin))
        self.run_tasks(tasks)

    def stage_ada_sharded(self):
        P, I, nc = self.P, self.I, self.nc
        cv = self.small[:, 0:32].rearrange("p (k m) -> p k m", m=2)
        sil = self.silb[:, :].rearrange("p (k m) -> p k m", m=2)
        bt = self.small[:, 32:80].rearrange("p (l j) -> p l j", l=4)
        mo = P.sb("mown", [128, 4, 12, 2], F32)
        own = nc.dram_tensor("mods_own", [128, 96], F32).ap()
        allm = nc.dram_tensor("mods_all", [8 * 128, 96], F32).ap()
        P.dma("sp", cv, I["cvec"], slot="c1", writes=["small"])
        P.dma("sp", bt, I["b_adaT_sh"], slot="c1", writes=["bt"])
        P.op("act", lambda e: e.activation(out=sil, in_=cv, func=AF.Silu), reads=["small"], writes=["sil"])
        ps, psn = self.bank()
        tasks = []
        for l in range(4):
            wv = I["w_ada_sh"][l].rearrange("(c p) n -> p c n", p=128)
            for jp in range(6):
                def comp(view, wres, l=l, jp=jp):
                    for jj in range(2):
                        col = 2 * (l * 12 + 2 * jp + jj)
                        for k in range(KC):
                            P.op("pe", lambda e, k=k, jj=jj, col=col: e.matmul(
                                ps[:, col:col + 2], view[:, k, jj * 128:(jj + 1) * 128], sil[:, k, :],
                                start=(k == 0), stop=(k == KC - 1)), reads=[wres, "sil"], writes=[psn])
                tasks.append(((wv[:, :, jp * 256:(jp + 1) * 256], [128, KC, 256]), comp))
        self.run_tasks(tasks)
        pv = ps[:, 0:96].rearrange("p (l j m) -> p l j m", l=4, m=2)
        for m in range(2):
            P.op("dve", lambda e, m=m: e.tensor_tensor(out=mo[:, :, :, m], in0=pv[:, :, :, m], in1=bt, op=ALU.add),
                 reads=[psn, "bt"], writes=["mown"])
        P.dma("sp", own, mo[:].rearrange("p l j m -> p (l j m)"), slot="c2", reads=["mown"], writes=["dram_mown"])
        P.op("pool", lambda e: e.collective_compute("AllGather", ALU.bypass, replica_groups=[list(range(NCORES))],
                                                    ins=[own.opt()], outs=[allm.opt()]),
             reads=["dram_mown"], writes=["dram_mall"])
        for r in range(NCORES):
            for l in range(4):
                P.dma("sp", self.mods[:, l, r * 12:(r + 1) * 12, :],
                      allm[r * 128:(r + 1) * 128, l * 24:(l + 1) * 24].rearrange("p (j m) -> p j m", m=2),
                      slot="c2", reads=["dram_mall"], writes=["mods"])

    def layer_vecs(self, l):
        P = self.P
        for which, joff in ((0, 16), (1, 64)):
            for m in range(2):
                P.op("dve", lambda e, which=which, joff=joff, m=m: e.scalar_tensor_tensor(
                    out=self.vecA[:, which, :, m], in0=self.mods[:, l, joff:joff + 16, m], scalar=1.0,
                    in1=self.nT[:, which, l, :], op0=ALU.add, op1=ALU.mult),
                    reads=["mods", "nT"], writes=["vecA"])

    def rsqrt_inplace(self, ap, res):
        self.P.op("dve", lambda e: e.reciprocal(out=ap, in_=ap), reads=[res], writes=[res])
        self.P.op("act", lambda e: e.activation(out=ap, in_=ap, func=AF.Sqrt), reads=[res], writes=[res])

    def modv(self, l, j, c, m):
        return self.mods[:, l, j * 16 + c, m:m + 1]

    def stage_norm(self, l, which, tiles, final=False, outT=None):
        P = self.P
        for t in tiles:
            t0, tw = TILES[t]
            m = 1 if t == 2 else 0
            ps, psn = self.bank()
            for c in range(KC):
                si = self.rot("sq", 2)
                sq = self.sqb[si]
                P.op("act", lambda e, c=c, sq=sq: e.activation(out=sq[:, :tw], in_=self.x[:, c, t0:t0 + tw],
                                                              func=AF.Square),
                     reads=["x%d.%d" % (c, t)], writes=["sq%d" % si])
                P.op("pe", lambda e, c=c, sq=sq: e.matmul(ps[:, :tw], self.ones, sq[:, :tw], start=(c == 0),
                                                         stop=(c == KC - 1)),
                     reads=["sq%d" % si, "cm"], writes=[psn])
            rs = self.rstd[:, t0:t0 + tw]
            P.op("dve", lambda e: e.tensor_scalar(out=rs, in0=ps[:, :tw], scalar1=1.0 / D, scalar2=EPS,
                                                 op0=ALU.mult, op1=ALU.add), reads=[psn], writes=["rstd%d" % t])
            self.rsqrt_inplace(rs, "rstd%d" % t)
            for c in range(KC):
                tb, tn = self.tmp()
                if final:
                    A = self.I_fn[:, c:c + 1]
                else:
                    A = self.vecA[:, which, c, m:m + 1]
                P.op("dve", lambda e, c=c, tb=tb, A=A: e.scalar_tensor_tensor(
                    out=tb[:, :tw], in0=self.x[:, c, t0:t0 + tw], scalar=A, in1=rs, op0=ALU.mult, op1=ALU.mult),
                    reads=["x%d.%d" % (c, t), "rstd%d" % t, "vecA", "fn"], writes=[tn])
                if final:
                    P.dma("sp", outT[:, c, t0:t0 + tw], tb[:, :tw], slot="o%d" % (self.rot("oslot", 4)), reads=[tn])
                else:
                    B = self.modv(l, 0 if which == 0 else 3, c, m)
                    P.op("act", lambda e, c=c, tb=tb, B=B: e.activation(
                        out=self.hv[:, c, t0:t0 + tw], in_=tb[:, :tw], func=AF.Identity, bias=B, scale=1.0),
                        reads=[tn, "mods"], writes=["h%d.%d" % (c, t)])

    def x_accum(self, l, gj, oc, t, ps, psn, tw):
        t0, _ = TILES[t]
        m = 1 if t == 2 else 0
        G = self.modv(l, gj, oc, m)
        self.P.op("dve", lambda e: e.scalar_tensor_tensor(
            out=self.x[:, oc, t0:t0 + tw], in0=ps[:, :tw], scalar=G, in1=self.x[:, oc, t0:t0 + tw],
            op0=ALU.mult, op1=ALU.add), reads=[psn, "mods", "x%d.%d" % (oc, t)], writes=["x%d.%d" % (oc, t)])

    def stage_mlp(self, l, tiles):
        P, I = self.P, self.I
        self.stage_norm(l, 1, tiles)
        h1 = self.aux[:, :].rearrange("p (c t) -> p c t", c=8)
        li = self.mlp_layers.index(l)
        w1 = I["w_mlp1"][li].rearrange("(c p) n -> p c n", p=128)
        w2 = I["w_mlp2"][li].rearrange("(c p) n -> p c n", p=128)
        tasks = []
        for g in range(8):
            for pp in range(4):
                def c1(view, wres, pp=pp):
                    for fi in range(2):
                        f = 2 * pp + fi
                        for t in tiles:
                            t0, tw = TILES[t]
                            ps, psn = self.bank()
                            for k in range(KC):
                                P.op("pe", lambda e, k=k, fi=fi, ps=ps: e.matmul(
                                    ps[:, :tw], view[:, k, fi * 128:(fi + 1) * 128], self.hv[:, k, t0:t0 + tw],
                                    start=(k == 0), stop=(k == KC - 1)),
                                    reads=[wres, "h%d.%d" % (k, t)], writes=[psn])
                            tb, tn = self.tmp()
                            P.op("act", lambda e, ps=ps, tb=tb: e.activation(out=tb[:, :tw], in_=ps[:, :tw],
                                                                          func=AF.Relu), reads=[psn], writes=[tn])
                            P.op("pool", lambda e, tb=tb, f=f: e.tensor_tensor(
                                out=h1[:, f, t0:t0 + tw], in0=tb[:, :tw], in1=tb[:, :tw], op=ALU.mult),
                                reads=[tn], writes=["a%d.%d" % (f, t)])
                col = g * 1024 + pp * 256
                tasks.append(((w1[:, :, col:col + 256], [128, KC, 256]), c1))
            for op4 in range(4):
                def c2(view, wres, op4=op4):
                    for oi in range(4):
                        oc = 4 * op4 + oi
                        for t in tiles:
                            t0, tw = TILES[t]
                            ps, psn = self.bank()
                            for kk in range(8):
                                P.op("pe", lambda e, kk=kk, oi=oi, ps=ps: e.matmul(
                                    ps[:, :tw], view[:, kk, oi * 128:(oi + 1) * 128], h1[:, kk, t0:t0 + tw],
                                    start=(kk == 0), stop=(kk == 7)),
                                    reads=[wres, "a%d.%d" % (kk, t)], writes=[psn])
                            self.x_accum(l, 5, oc, t, ps, psn, tw)
                tasks.append(((w2[:, 8 * g:8 * g + 8, op4 * 512:(op4 + 1) * 512], [128, 8, 512]), c2))
        self.run_tasks(tasks)

    def stage_tables(self, blk):
        self.P.dma("sp", self.tab[:], self.I["ropetab"][0 if blk == 128 else 1], slot="c1", reads=["tab"],
                   writes=["tab"] + self.TP)

    def head_post(self, ps, psn, t, rows, gain, norm_n, rotm, dst, rope=True, dres="dram_kv"):
        P = self.P
        t0, tw = TILES[t]
        latent = (t != 2) and rope
        pv = ps[0:rows, :tw]
        if gain is None and norm_n is None and not latent:
            gi = self.rot("stg", 2)
            sg = self.stg[gi]
            P.op("act", lambda e: e.activation(out=sg[0:rows, :tw], in_=pv, func=AF.Copy), reads=[psn],
                 writes=["stg%d" % gi])
            P.dma("sp", dst, sg[0:rows, :tw], slot="st%d" % gi, reads=["stg%d" % gi], writes=[dres])
            return
        qg, qgn = self.tmp()
        if gain is not None:
            P.op("act", lambda e: e.activation(out=qg[0:rows, :tw], in_=pv, func=AF.Copy, scale=gain),
                 reads=[psn, "small"], writes=[qgn])
        else:
            P.op("act", lambda e: e.activation(out=qg[0:rows, :tw], in_=pv, func=AF.Copy),
                 reads=[psn], writes=[qgn])
        rsn = None
        if norm_n is not None:
            si = self.rot("sq", 2)
            sq = self.sqb[si]
            P.op("act", lambda e: e.activation(out=sq[0:rows, :tw], in_=pv, func=AF.Square),
                 reads=[psn], writes=["sq%d" % si])
            p2, p2n = self.bank()
            P.op("pe", lambda e: e.matmul(p2[0:rows, :tw], self.ones[0:rows, 0:rows], sq[0:rows, :tw],
                                          start=True, stop=True), reads=["sq%d" % si, "cm"], writes=[p2n])
            rsb, rsn = self.tmp()
            P.op("dve", lambda e: e.tensor_scalar(out=rsb[0:rows, :tw], in0=p2[0:rows, :tw], scalar1=1.0 / norm_n,
                                                 scalar2=EPS, op0=ALU.mult, op1=ALU.add), reads=[p2n], writes=[rsn])
            self.rsqrt_inplace(rsb[0:rows, :tw], rsn)
        gi = self.rot("stg", 2)
        sg = self.stg[gi]
        sgn = "stg%d" % gi
        if latent:
            bi = self.rot("pt", 4)
            qb = self.ptb[bi]
            P.op("pool", lambda e: e.tensor_copy(out=qb[0:rows, :tw], in_=qg[0:rows, :tw]),
                 reads=[qgn], writes=["pt%d" % bi])
            p3, p3n = self.bank()
            P.op("pe", lambda e: e.matmul(p3[0:rows, :tw], rotm[0:rows, 0:rows], qb[0:rows, :tw],
                                          start=True, stop=True), reads=["pt%d" % bi, "cm"], writes=[p3n])
            t2, t2n = self.tmp()
            P.op("dve", lambda e: e.tensor_tensor(out=t2[0:rows, :tw], in0=p3[0:rows, :tw],
                                                 in1=self.tab[0:rows, 1, t0:t0 + tw], op=ALU.mult),
                 reads=[p3n, "tab"], writes=[t2n])
            P.op("dve", lambda e: e.tensor_tensor(out=qg[0:rows, :tw], in0=qg[0:rows, :tw],
                                                 in1=self.tab[0:rows, 0, t0:t0 + tw], op=ALU.mult),
                 reads=[qgn, "tab"], writes=[qgn])
            if rsn is not None:
                P.op("dve", lambda e: e.tensor_tensor(out=qg[0:rows, :tw], in0=qg[0:rows, :tw], in1=t2[0:rows, :tw],
                                                     op=ALU.add), reads=[qgn, t2n], writes=[qgn])
                P.op("dve", lambda e: e.tensor_tensor(out=sg[0:rows, :tw], in0=qg[0:rows, :tw], in1=rsb[0:rows, :tw],
                                                     op=ALU.mult), reads=[qgn, rsn], writes=[sgn])
            else:
                P.op("dve", lambda e: e.tensor_tensor(out=sg[0:rows, :tw], in0=qg[0:rows, :tw], in1=t2[0:rows, :tw],
                                                     op=ALU.add), reads=[qgn, t2n], writes=[sgn])
        else:
            if rsn is not None:
                P.op("dve", lambda e: e.tensor_tensor(out=sg[0:rows, :tw], in0=qg[0:rows, :tw], in1=rsb[0:rows, :tw],
                                                     op=ALU.mult), reads=[qgn, rsn], writes=[sgn])
            else:
                P.op("dve", lambda e: e.tensor_copy(out=sg[0:rows, :tw], in_=qg[0:rows, :tw]),
                     reads=[qgn], writes=[sgn])
        P.dma("sp", dst, sg[0:rows, :tw], slot="st%d" % gi, reads=[sgn], writes=[dres])

    def v_tiles(self, view, wres, src, srcname, nk, ncols, dst_fn):
        P = self.P
        for jt in range(10):
            t = 0 if jt < 4 else (1 if jt < 8 else 2)
            ps, psn = self.bank()
            for k in range(nk):
                P.op("pe", lambda e, k=k, ps=ps: e.matmul(ps[:, :ncols], src(k, jt), view[:, k, 0:ncols],
                                                         start=(k == 0), stop=(k == nk - 1)),
                     reads=[wres, srcname(k, t)], writes=[psn])
            gi = self.rot("stg", 2)
            sg = self.stg[gi]
            P.op("act", lambda e, ps=ps, sg=sg: e.activation(out=sg[:, :ncols], in_=ps[:, :ncols], func=AF.Copy),
                 reads=[psn], writes=["stg%d" % gi])
            P.dma("sp", dst_fn(jt), sg[:, :ncols].rearrange("p (g d) -> p g d", d=128), slot="st%d" % gi,
                  reads=["stg%d" % gi], writes=["dram_kv"])

    def stage_gqa_proj(self, l, j, L, ctx_q, exch=None):
        P, I, Dm = self.P, self.I, self.Dm
        self.stage_tables(128)
        gq = self.small[:, 128:130]
        P.dma("sp", gq, I["a_qk"][:, j, :], slot="c2", reads=["small"], writes=["small"])
        wv = I["a_w_qkv"][j].rearrange("(c p) n -> p c n", p=128)
        kvo, kvc, qs = Dm["kvo%d" % L], Dm["kvc%d" % L], Dm["qs%d" % L]
        tasks = []

        def proj_heads(view, wres, heads, gain, dst_of, tiles, dres):
            for hi, hd in enumerate(heads):
                for t in tiles:
                    t0, tw = TILES[t]
                    ps, psn = self.bank()
                    for k in range(KC):
                        P.op("pe", lambda e, k=k, ps=ps, hi=hi: e.matmul(
                            ps[:, :tw], view[:, k, hi * 128:(hi + 1) * 128], self.hv[:, k, t0:t0 + tw],
                            start=(k == 0), stop=(k == KC - 1)), reads=[wres, "h%d.%d" % (k, t)], writes=[psn])
                    self.head_post(ps, psn, t, 128, gain, 128, self.rot128, dst_of(hd, t), dres=dres)

        def kdst(g, t):
            t0, tw = TILES[t]
            if t == 2:
                return kvc[g * 128:(g + 1) * 128, :]
            return kvo[g * 128:(g + 1) * 128, t0:t0 + tw]

        def qdst(hd, t):
            t0, tw = TILES[t]
            return qs[hd * 128:(hd + 1) * 128, t0:t0 + tw]

        for kp in range(2):
            tasks.append(((wv[:, :, 2048 + kp * 256:2048 + (kp + 1) * 256], [128, KC, 256]),
                          lambda v, r, kp=kp: proj_heads(v, r, [2 * kp, 2 * kp + 1], gq[:, 1:2], kdst, [0, 1, 2], "dram_kv")))
        for vp in range(2):
            def vdst(jt, vp=vp):
                if jt < 8:
                    return kvo[512 + vp * 256:512 + (vp + 1) * 256, jt * 128:(jt + 1) * 128].rearrange(
                        "(g p) d -> p g d", p=128)
                return kvc[512 + vp * 256:512 + (vp + 1) * 256, (jt - 8) * 128:(jt - 7) * 128].rearrange(
                    "(g p) d -> p g d", p=128)
            tasks.append(((wv[:, :, 2560 + vp * 256:2560 + (vp + 1) * 256], [128, KC, 256]),
                          lambda v, r, vdst=vdst: self.v_tiles(
                              v, r, lambda k, jt: self.hv[:, k, jt * 128:(jt + 1) * 128],
                              lambda k, t: "h%d.%d" % (k, t), KC, 256, vdst)))
        if exch is not None:
            tasks.append((None, exch))
        qt = [0, 1, 2] if ctx_q else [0, 1]
        for qp in range(8):
            tasks.append(((wv[:, :, qp * 256:(qp + 1) * 256], [128, KC, 256]),
                          lambda v, r, qp=qp: proj_heads(v, r, [2 * qp, 2 * qp + 1], gq[:, 0:1], qdst, qt, "dram_q")))
        self.run_tasks(tasks)

    def exchange(self, own, allb, name):
        P = self.P
        P.op("pool", lambda e: e.collective_compute("AllGather", ALU.bypass, replica_groups=[list(range(NCORES))],
                                                    ins=[own.opt()], outs=[allb.opt()]),
             reads=["dram_kv", "dram_halo"], writes=["dram_all_" + name])

    @staticmethod
    def kvh(ck):
        return "kvA" if ck < 32 else "kvB"

    def kv_load(self, Kb, Vb, kva3, kvc, kr0, vr0, L):
        P = self.P
        hres = ["h%d.%d" % (c, t) for c in range(KC) for t in range(3)]
        ra = ["dram_all_kv%d" % L]
        wa, wb = hres + ["kvA"], hres + ["kvB"]
        P.dma("sp", Kb[:, 0:4096].rearrange("p (r t) -> p r t", r=4), kva3[kr0:kr0 + 128, 0:4, :], slot="kvA",
              reads=ra, writes=wa)
        P.dma("sp", Vb[:, 0:32, :].rearrange("p (r j) d -> p r (j d)", r=4), kva3[vr0:vr0 + 128, 0:4, :], slot="kvA",
              reads=ra, writes=wa)
        P.dma("sp", Kb[:, 4096:8192].rearrange("p (r t) -> p r t", r=4), kva3[kr0:kr0 + 128, 4:8, :], slot="kvB",
              reads=ra, writes=wb)
        P.dma("sp", Kb[:, 8192:NKEY], kvc[kr0:kr0 + 128, :], slot="kvB", reads=["dram_kv"], writes=wb)
        P.dma("sp", Vb[:, 32:64, :].rearrange("p (r j) d -> p r (j d)", r=4), kva3[vr0:vr0 + 128, 4:8, :], slot="kvB",
              reads=ra, writes=wb)
        P.dma("sp", Vb[:, 64:66, :].rearrange("p j d -> p (j d)"), kvc[vr0:vr0 + 128, :], slot="kvB",
              reads=["dram_kv"], writes=wb)

    def attn_unit(self, q0, qw, chunks, kq, vch, scale, out_ap, extra_s, out_res):
        P = self.P
        SB = (self.psb[0], self.psb[1], self.psb[2], self.psb[7])
        SBI = (0, 1, 2, 7)
        oi = 3 + 2 * self.rot("ob", 2)
        psO, psS = self.psb[oi], self.psb[oi + 1]
        n = len(chunks)
        assert n % 2 == 0
        ptl = [(self.ptb[i], "pt%d" % i) for i in range(4)] + [(self.tpb[i], "tp%d" % i) for i in range(2)]
        sml = [(self.tpb[2 + i], "tp%d" % (2 + i)) for i in range(6)]

        def emit_s(i):
            bi = self.rot("sb", 4)
            prs = kq(chunks[i])
            for pi, (lt, rh) in enumerate(prs):
                P.op("pe", lambda e, lt=lt, rh=rh, pi=pi: e.matmul(
                    SB[bi][:, :qw], lt, rh, start=(pi == 0), stop=(pi == len(prs) - 1)),
                    reads=[self.kvh(chunks[i])] + extra_s, writes=["ps%d" % SBI[bi]])
            return bi

        groups = [list(range(g0, min(g0 + 8, n))) for g0 in range(0, n, 8)]
        ng = len(groups)
        sums = {}

        def add(eng, dst, a_, b_):
            (d, dn), (x_, xn), (y_, yn) = dst, a_, b_
            P.op(eng, lambda e: e.tensor_tensor(out=d[:, :qw], in0=x_[:, :qw], in1=y_[:, :qw], op=ALU.add),
                 reads=[xn, yn], writes=[dn])

        def after_exp(i):
            gi, k = i // 8, i % 8
            idx = groups[gi]
            if k % 2 == 1:
                t = sml[self.rot("asm", 6)]
                sums[(gi, k // 2)] = t
                add("dve", t, pts[i - 1], pts[i])
                if k % 4 == 3:
                    lo = sums[(gi, k // 2 - 1)]
                    add("dve", lo, lo, t)
                    if k == 7:
                        add("dve", sums[(gi, 0)], sums[(gi, 0)], sums[(gi, 2)])
            if i == idx[-1]:
                m = len(idx)
                if m == 2 or m == 4 or m == 8:
                    pass
                elif m == 6:
                    add("dve", sums[(gi, 0)], sums[(gi, 0)], sums[(gi, 2)])
                else:
                    raise AssertionError(m)

        def emit_sum(gi):
            sa, san = sums[(gi, 0)]
            P.op("pe", lambda e: e.matmul(psS[:, :qw], self.ones, sa[:, :qw], start=(gi == 0), stop=(gi == ng - 1)),
                 reads=[san, "cm"], writes=["ps%d" % (oi + 1)])

        sbk = {}
        pts = {}
        for i in range(min(3, n)):
            sbk[i] = emit_s(i)
        gdone = 0
        for i in range(n):
            bi = sbk.pop(i)
            pt, ptn = ptl[self.rot("apt", 6)]
            pts[i] = (pt, ptn)
            P.op("act", lambda e, pt=pt: e.activation(out=pt[:, :qw], in_=SB[bi][:, :qw], func=AF.Exp, scale=scale),
                 reads=["ps%d" % SBI[bi]], writes=[ptn])
            P.op("pe", lambda e, pt=pt: e.matmul(psO[:, :qw], vch(chunks[i]), pt[:, :qw], start=(i == 0),
                                                 stop=(i == n - 1)),
                 reads=[self.kvh(chunks[i]), ptn], writes=["ps%d" % oi])
            after_exp(i)
            if i + 3 < n:
                sbk[i + 3] = emit_s(i + 3)
            if gdone < ng and i >= groups[gdone][-1] + 1:
                emit_sum(gdone)
                gdone += 1
        while gdone < ng:
            emit_sum(gdone)
            gdone += 1
        rc, rcn = self.tmp()
        P.op("dve", lambda e: e.reciprocal(out=rc[:, :qw], in_=psS[:, :qw]), reads=["ps%d" % (oi + 1)], writes=[rcn])
        P.op("dve", lambda e: e.tensor_tensor(out=out_ap, in0=psO[:, :qw], in1=rc[:, :qw], op=ALU.mult),
             reads=["ps%d" % oi, rcn], writes=[out_res])

    def wo_tasks(self, l, wo, g, tiles, Og):
        P = self.P
        tasks = []
        wv = wo.rearrange("(c p) n -> p c n", p=128)
        for half in range(2):
            def comp(view, wres, half=half):
                for oi in range(8):
                    oc = 8 * half + oi
                    for t in tiles:
                        t0, tw = TILES[t]
                        bi = (0, 1, 2, 7)[self.rot("wob", 4)]
                        ps, psn = self.psb[bi], "ps%d" % bi
                        for kk in range(4):
                            P.op("pe", lambda e, kk=kk, oi=oi, ps=ps: e.matmul(
                                ps[:, :tw], view[:, kk, oi * 128:(oi + 1) * 128], Og[:, kk, t0:t0 + tw],
                                start=(kk == 0), stop=(kk == 3)), reads=[wres, "ao"], writes=[psn])
                        self.x_accum(l, 2, oc, t, ps, psn, tw)
            tasks.append(((wv[:, 4 * g:4 * g + 4, half * 1024:(half + 1) * 1024], [128, 4, 1024]), comp))
        return tasks

    def stage_gqa_attn(self, l, j, L, ctx_out):
        P, I, Dm = self.P, self.I, self.Dm
        kva, kvc, qs = Dm["kva%d" % L], Dm["kvc%d" % L], Dm["qs%d" % L]
        Kb = self.h[:, 0:NKEY]
        Vb = self.h[:, NKEY:2 * NKEY].rearrange("p (c d) -> p c d", d=128)
        A4 = self.aux[:, :].rearrange("p (c t) -> p c t", c=8)
        Qg, Og = A4[:, 0:4, :], A4[:, 4:8, :]
        kva3 = kva.rearrange("(r n) t -> n r t", r=8)
        scale = 128 ** -0.5
        hres = ["h%d.%d" % (c, t) for c in range(KC) for t in range(3)]
        tiles = [0, 1, 2] if ctx_out else [0, 1]
        for g in range(4):
            self.kv_load(Kb, Vb, kva3, kvc, g * 128, 512 + g * 128, L)
            P.dma("sp", Qg, qs[g * 512:(g + 1) * 512, :].rearrange("(h p) t -> p h t", p=128), slot="q",
                  reads=["dram_q"], writes=["aq"])
            for hh in range(4):
                for t in tiles:
                    t0, tw = TILES[t]
                    chunks = list(range(NCH)) if t != 2 else [64, 65]
                    self.attn_unit(
                        t0, tw, chunks,
                        lambda ck, hh=hh, t0=t0, tw=tw: [(Kb[:, ck * 128:(ck + 1) * 128], Qg[:, hh, t0:t0 + tw])],
                        lambda ck: Vb[:, ck, :], scale, Og[:, hh, t0:t0 + tw], ["aq"], "ao")
            self.run_tasks(self.wo_tasks(l, I["a_w_o"][j], g, tiles, Og))

    def stage_conv_main(self, l):
        P, I, Dm = self.P, self.I, self.Dm
        win = I["b_w_in"].rearrange("(c p) n -> p c n", p=128)
        wout = I["b_w_out"].rearrange("(c p) n -> p c n", p=128)
        cw = self.small[:, 64:112].rearrange("p (w c) -> p w c", w=3)
        P.dma("sp", cw, I["b_convT"], slot="c2", reads=["small"], writes=["small"])
        mb = self.mbuf
        P.op("pool", lambda e: e.memset(mb[:, :], 0.0), reads=["tab"], writes=["tab"] + self.TP)
        bgh = self.small[:, 0:32].rearrange("p (c m) -> p c m", m=2)
        mh = self.small[:, 32:64].rearrange("p (c m) -> p c m", m=2)
        gb = self.aux[:, 0:2 * NT].rearrange("p (c t) -> p c t", c=2)
        moff = [1, 1, 3]
        stash = {}

        def m_compute(c, vcg, rcg, vu, ru):
            for t in range(3):
                t0, tw = TILES[t]
                pa, pan = self.bank()
                for k in range(KC):
                    P.op("pe", lambda e, k=k, pa=pa: e.matmul(pa[:, :tw], vcg[:, k, :], self.hv[:, k, t0:t0 + tw],
                                                             start=(k == 0), stop=(k == KC - 1)),
                         reads=[rcg, "h%d.%d" % (k, t)], writes=[pan])
                tb, tn = self.tmp()
                P.op("act", lambda e, pa=pa, tb=tb: e.activation(out=tb[:, :tw], in_=pa[:, :tw], func=AF.Copy),
                     reads=[pan], writes=[tn])
                pb, pbn = self.bank()
                for k in range(KC):
                    P.op("pe", lambda e, k=k, pb=pb: e.matmul(pb[:, :tw], vu[:, k, :], self.hv[:, k, t0:t0 + tw],
                                                             start=(k == 0), stop=(k == KC - 1)),
                         reads=[ru, "h%d.%d" % (k, t)], writes=[pbn])
                o = moff[t] + t0
                P.op("dve", lambda e, pb=pb, tb=tb, o=o: e.tensor_tensor(out=mb[:, o:o + tw], in0=pb[:, :tw],
                                                                        in1=tb[:, :tw], op=ALU.mult),
                     reads=[pbn, tn], writes=["tab"])
            if "c_notiny" in DBG:
                return
            P.op("dve", lambda e: e.tensor_copy(out=mh[:, c, 0:1], in_=mb[:, 1:2]), reads=["tab"], writes=["small"])
            P.op("dve", lambda e: e.tensor_copy(out=mh[:, c, 1:2], in_=mb[:, 1024:1025]), reads=["tab"],
                 writes=["small"])

        def gate(c, vbg, rbg):
            if "c_nogate" in DBG:
                return
            for t in range(3):
                t0, tw = TILES[t]
                pc, pcn = self.bank()
                for k in range(KC):
                    P.op("pe", lambda e, k=k, pc=pc: e.matmul(pc[:, :tw], vbg[:, k, :], self.hv[:, k, t0:t0 + tw],
                                                             start=(k == 0), stop=(k == KC - 1)),
                         reads=[rbg, "h%d.%d" % (k, t)], writes=[pcn])
                o = moff[t] + t0
                cvb, cvn = self.tmp()
                P.op("dve", lambda e, cvb=cvb, o=o: e.tensor_scalar(
                    out=cvb[:, :tw], in0=mb[:, o - 1:o - 1 + tw], scalar1=cw[:, 0, c:c + 1], scalar2=None,
                    op0=ALU.mult), reads=["tab", "small"], writes=[cvn])
                for wi in (1, 2):
                    P.op("dve", lambda e, cvb=cvb, o=o, wi=wi: e.scalar_tensor_tensor(
                        out=cvb[:, :tw], in0=mb[:, o - 1 + wi:o - 1 + wi + tw], scalar=cw[:, wi, c:c + 1],
                        in1=cvb[:, :tw], op0=ALU.mult, op1=ALU.add), reads=["tab", "small", cvn], writes=[cvn])
                P.op("dve", lambda e, cvb=cvb, pc=pc: e.tensor_tensor(
                    out=gb[:, c % 2, t0:t0 + tw], in0=pc[:, :tw], in1=cvb[:, :tw], op=ALU.mult),
                    reads=[pcn, cvn], writes=["ao"])
                if "c_notiny" in DBG:
                    continue
                if t == 0:
                    P.op("dve", lambda e, pc=pc: e.tensor_copy(out=bgh[:, c, 0:1], in_=pc[:, 0:1]),
                         reads=[pcn], writes=["small"])
                if t == 1:
                    P.op("dve", lambda e, pc=pc: e.tensor_copy(out=bgh[:, c, 1:2], in_=pc[:, 511:512]),
                         reads=[pcn], writes=["small"])

        def outp(half, view, wres):
            if "c_noout" in DBG:
                return
            for oi in range(8):
                oc = 8 * half + oi
                for t in range(3):
                    t0, tw = TILES[t]
                    ps, psn = self.bank()
                    for kk in range(2):
                        P.op("pe", lambda e, kk=kk, oi=oi, ps=ps: e.matmul(
                            ps[:, :tw], view[:, kk, oi * 128:(oi + 1) * 128], gb[:, kk, t0:t0 + tw],
                            start=(kk == 0), stop=(kk == 1)), reads=[wres, "ao"], writes=[psn])
                    self.x_accum(l, 2, oc, t, ps, psn, tw)

        tasks = []
        for c in range(KC):
            tasks.append(((win[:, :, 2048 + c * 128:2048 + (c + 1) * 128], [128, KC, 128]),
                          lambda v, r: stash.__setitem__("cg", (v, r))))
            tasks.append(((win[:, :, 4096 + c * 128:4096 + (c + 1) * 128], [128, KC, 128]),
                          lambda v, r, c=c: m_compute(c, stash["cg"][0], stash["cg"][1], v, r)))
            tasks.append(((win[:, :, c * 128:(c + 1) * 128], [128, KC, 128]), lambda v, r, c=c: gate(c, v, r)))
            if c % 2 == 1:
                cp = c // 2
                for half in range(2):
                    tasks.append(((wout[:, 2 * cp:2 * cp + 2, half * 1024:(half + 1) * 1024], [128, 2, 1024]),
                                  lambda v, r, half=half: outp(half, v, r)))
        self.run_tasks(tasks, hold=1)
        P.dma("sp", Dm["halo_o"].rearrange("p (c m) -> p c m", m=2), mh, slot="hs", reads=["small"],
              writes=["dram_halo"])
        if 2 not in self.segs:
            P.dma("sp", Dm["bgh"], bgh, slot="hs", reads=["small"])

    def stage_conv_fix(self, l):
        P, I, Dm = self.P, self.I, self.Dm
        wout = I["b_w_out"].rearrange("(c p) n -> p c n", p=128)
        cw = self.small[:, 64:112].rearrange("p (w c) -> p w c", w=3)
        bgh = self.small[:, 0:32].rearrange("p (c m) -> p c m", m=2)
        lr = self.small[:, 32:64].rearrange("p (c m) -> p c m", m=2)
        if 1 not in self.segs:
            P.dma("sp", cw, I["b_convT"], slot="c2", reads=["small"], writes=["small"])
            P.dma("sp", bgh, Dm["bgh"], slot="c2", reads=["small"], writes=["small"])
        if "halo_lr" in I:
            P.dma("sp", lr, I["halo_lr"], slot="c2", reads=["small"], writes=["small"])
        else:
            hm = self.small[:, 112:128].rearrange("p (m r) -> p m r", m=2)
            P.dma("sp", hm, I["hmask"], slot="c2", reads=["small"], writes=["small"])
            ga = self.tab[:, 0, 0:256].rearrange("p (r c m) -> p r c m", r=8, m=2)
            P.dma("sp", ga, Dm["halo_a"].rearrange("(r p) (c m) -> p r c m", p=128, m=2), slot="c2",
                  reads=["dram_all_halo", "tab"], writes=["tab"] + self.TP)
            for side, mi in ((0, 1), (1, 0)):
                for r in range(8):
                    if r == 0:
                        P.op("dve", lambda e, side=side, mi=mi, r=r: e.tensor_scalar(
                            out=lr[:, :, side], in0=ga[:, r, :, mi], scalar1=hm[:, side, r:r + 1], scalar2=None,
                            op0=ALU.mult), reads=["tab", "small"], writes=["small"])
                    else:
                        P.op("dve", lambda e, side=side, mi=mi, r=r: e.scalar_tensor_tensor(
                            out=lr[:, :, side], in0=ga[:, r, :, mi], scalar=hm[:, side, r:r + 1], in1=lr[:, :, side],
                            op0=ALU.mult, op1=ALU.add), reads=["tab", "small"], writes=["small"])
        dg = self.sqb[0][:, 0:32].rearrange("p (c m) -> p c m", m=2)
        tf = self.small[:, 128:160].rearrange("p (c m) -> p c m", m=2)
        for side, wi in ((0, 0), (1, 2)):
            P.op("dve", lambda e, side=side, wi=wi: e.tensor_tensor(out=tf[:, :, side], in0=bgh[:, :, side],
                                                                   in1=cw[:, wi, :], op=ALU.mult),
                 reads=["small"], writes=["small"])
            P.op("dve", lambda e, side=side: e.tensor_tensor(out=dg[:, :, side], in0=tf[:, :, side],
                                                            in1=lr[:, :, side], op=ALU.mult),
                 reads=["small"], writes=["sq0"])
        ps, psn = self.bank()
        tasks = []
        for op8 in range(8):
            def comp(view, wres, op8=op8):
                for oi in range(2):
                    oc = 2 * op8 + oi
                    for k in range(KC):
                        P.op("pe", lambda e, k=k, oi=oi, oc=oc: e.matmul(
                            ps[:, 2 * oc:2 * oc + 2], view[:, k, oi * 128:(oi + 1) * 128], dg[:, k, :],
                            start=(k == 0), stop=(k == KC - 1)), reads=[wres, "sq0"], writes=[psn])
            tasks.append(((wout[:, :, op8 * 256:(op8 + 1) * 256], [128, KC, 256]), comp))
        self.run_tasks(tasks)
        pv = ps[:, 0:32].rearrange("p (c m) -> p c m", m=2)
        for side, col in ((0, 0), (1, 1023)):
            P.op("dve", lambda e, side=side: e.tensor_tensor(out=tf[:, :, side], in0=pv[:, :, side],
                                                            in1=self.mods[:, l, 32:48, 0], op=ALU.mult),
                 reads=[psn, "mods", "small"], writes=["small"])
            t = 0 if col == 0 else 1
            P.op("dve", lambda e, side=side, col=col: e.tensor_tensor(
                out=self.x[:, :, col], in0=self.x[:, :, col], in1=tf[:, :, side], op=ALU.add),
                reads=["small"] + ["x%d.%d" % (c, t) for c in range(KC)],
                writes=["x%d.%d" % (c, t) for c in range(KC)])

    def stage_mla_proj(self, l, L, exch=None):
        P, I, Dm = self.P, self.I, self.Dm
        self.stage_tables(64)
        kvo, kvc, qs = Dm["kvo%d" % L], Dm["kvc%d" % L], Dm["qs%d" % L]
        cn = self.small[:, 128:136].rearrange("p (a c) -> p a c", a=2)
        P.dma("sp", cn, I["c_nT"], slot="c2", reads=["small"], writes=["small"])
        A4 = self.aux[:, :].rearrange("p (c t) -> p c t", c=8)
        cq, ckv = A4[:, 0:4, :], A4[:, 4:8, :]
        wdq = I["c_w_dq"].rearrange("(c p) n -> p c n", p=128)
        wdkv = I["c_w_dkv"].rearrange("(c p) n -> p c n", p=128)
        tasks = []
        qtasks = []
        pn = {}

        def down(view, wres, dstb, pp, which, key):
            for ci in range(2):
                ch = 2 * pp + ci
                for t in range(3):
                    t0, tw = TILES[t]
                    ps, psn = self.psb[3 + (self.rot("dn", 2))], None
                    bi = 3 + ((self.slot_n["dn"] - 1) % 2)
                    psn = "ps%d" % bi
                    for k in range(KC):
                        P.op("pe", lambda e, k=k, ps=ps, ci=ci: e.matmul(
                            ps[:, :tw], view[:, k, ci * 128:(ci + 1) * 128], self.hv[:, k, t0:t0 + tw],
                            start=(k == 0), stop=(k == KC - 1)), reads=[wres, "h%d.%d" % (k, t)], writes=[psn])
                    si = self.rot("sq", 2)
                    sq = self.sqb[si]
                    P.op("act", lambda e, ps=ps, sq=sq: e.activation(out=sq[:, :tw], in_=ps[:, :tw], func=AF.Square),
                         reads=[psn], writes=["sq%d" % si])
                    P.op("act", lambda e, ps=ps, ch=ch: e.activation(out=dstb[:, ch, t0:t0 + tw], in_=ps[:, :tw],
                                                                    func=AF.Copy), reads=[psn], writes=[key])
                    acc = self.psb[t]
                    P.op("pe", lambda e, sq=sq, acc=acc, ch=ch: e.matmul(acc[:, :tw], self.ones, sq[:, :tw],
                                                                        start=(ch == 0), stop=(ch == 3)),
                         reads=["sq%d" % si, "cm"], writes=["ps%d" % t])

        def down_fin(dstb, which, key):
            for t in range(3):
                t0, tw = TILES[t]
                rs = self.rstd[:, t0:t0 + tw]
                acc = self.psb[t]
                P.op("dve", lambda e, acc=acc, rs=rs: e.tensor_scalar(out=rs, in0=acc[:, :tw], scalar1=1.0 / 512,
                                                                     scalar2=EPS, op0=ALU.mult, op1=ALU.add),
                     reads=["ps%d" % t], writes=["rstd%d" % t])
                self.rsqrt_inplace(rs, "rstd%d" % t)
                for ch in range(4):
                    P.op("dve", lambda e, ch=ch, rs=rs: e.scalar_tensor_tensor(
                        out=dstb[:, ch, t0:t0 + tw], in0=dstb[:, ch, t0:t0 + tw], scalar=cn[:, which, ch:ch + 1],
                        in1=rs, op0=ALU.mult, op1=ALU.mult), reads=[key, "rstd%d" % t, "small"], writes=[key])

        for pp in range(2):
            tasks.append(((wdq[:, :, pp * 256:(pp + 1) * 256], [128, KC, 256]),
                          lambda v, r, pp=pp: down(v, r, cq, pp, 0, "aq")))
        tasks.append((None, lambda: down_fin(cq, 0, "aq")))
        for pp in range(2):
            tasks.append(((wdkv[:, :, pp * 256:(pp + 1) * 256], [128, KC, 256]),
                          lambda v, r, pp=pp: down(v, r, ckv, pp, 1, "ao")))
        tasks.append((None, lambda: down_fin(ckv, 1, "ao")))

        def krope(view, wres):
            for t in range(3):
                t0, tw = TILES[t]
                ps, psn = self.bank()
                for k in range(KC):
                    P.op("pe", lambda e, k=k, ps=ps: e.matmul(ps[0:64, :tw], view[:, k, :], self.hv[:, k, t0:t0 + tw],
                                                             start=(k == 0), stop=(k == KC - 1)),
                         reads=[wres, "h%d.%d" % (k, t)], writes=[psn])
                dst = kvc[4096:4160, :] if t == 2 else kvo[4096:4160, t0:t0 + tw]
                self.head_post(ps, psn, t, 64, None, None, self.rot64, dst)
        tasks.append(((wdkv[:, :, 512:576], [128, KC, 64]), krope))

        wuq = I["c_w_uq"].rearrange("(c p) (h e) -> p c h e", p=128, e=192)
        for hp in range(2):
            def qn(view, wres, hp=hp):
                for hi in range(8):
                    hd = 8 * hp + hi
                    for t in range(3):
                        t0, tw = TILES[t]
                        ps, psn = self.bank()
                        for k in range(4):
                            P.op("pe", lambda e, k=k, ps=ps, hi=hi: e.matmul(
                                ps[:, :tw], view[:, k, hi, :], cq[:, k, t0:t0 + tw], start=(k == 0), stop=(k == 3)),
                                reads=[wres, "aq"], writes=[psn])
                        self.head_post(ps, psn, t, 128, None, None, None, qs[hd * 128:(hd + 1) * 128, t0:t0 + tw],
                                       rope=False, dres="dram_q")
            qtasks.append(((wuq[:, :, 8 * hp:8 * hp + 8, 0:128], [128, 4, 8, 128]), qn))

        def qr(view, wres):
            view = view.rearrange("p c h e -> p c (h e)")
            for hp2 in range(8):
                for t in range(3):
                    t0, tw = TILES[t]
                    ps, psn = self.bank()
                    for k in range(4):
                        P.op("pe", lambda e, k=k, ps=ps, hp2=hp2: e.matmul(
                            ps[:, :tw], view[:, k, hp2 * 128:(hp2 + 1) * 128], cq[:, k, t0:t0 + tw],
                            start=(k == 0), stop=(k == 3)), reads=[wres, "aq"], writes=[psn])
                    self.head_post(ps, psn, t, 128, None, None, self.rot64,
                                   qs[2048 + hp2 * 128:2048 + (hp2 + 1) * 128, t0:t0 + tw], dres="dram_q")
        qtasks.append(((wuq[:, :, :, 128:192], [128, 4, 16, 64]), qr))

        wukv = I["c_w_ukv"].rearrange("(c p) (h t d) -> p c h t d", p=128, t=2, d=128)
        for hp in range(2):
            def kn(view, wres, hp=hp):
                for hi in range(8):
                    hd = 8 * hp + hi
                    for t in range(3):
                        t0, tw = TILES[t]
                        ps, psn = self.bank()
                        for k in range(4):
                            P.op("pe", lambda e, k=k, ps=ps, hi=hi: e.matmul(
                                ps[:, :tw], view[:, k, hi, :], ckv[:, k, t0:t0 + tw], start=(k == 0), stop=(k == 3)),
                                reads=[wres, "ao"], writes=[psn])
                        dst = kvc[hd * 128:(hd + 1) * 128, :] if t == 2 else kvo[hd * 128:(hd + 1) * 128, t0:t0 + tw]
                        self.head_post(ps, psn, t, 128, None, None, None, dst, rope=False)
            tasks.append(((wukv[:, :, 8 * hp:8 * hp + 8, 0, :], [128, 4, 8, 128]), kn))
        for hq in range(4):
            def vdst(jt, hq=hq):
                r0 = 2048 + hq * 512
                if jt < 8:
                    return kvo[r0:r0 + 512, jt * 128:(jt + 1) * 128].rearrange("(g p) d -> p g d", p=128)
                return kvc[r0:r0 + 512, (jt - 8) * 128:(jt - 7) * 128].rearrange("(g p) d -> p g d", p=128)

            def vv(view, wres, vdst=vdst):
                v3 = view.rearrange("p c h d -> p c (h d)")
                self.v_tiles(v3, wres, lambda k, jt: ckv[:, k, jt * 128:(jt + 1) * 128], lambda k, t: "ao", 4, 512, vdst)
            tasks.append(((wukv[:, :, 4 * hq:4 * hq + 4, 1, :], [128, 4, 4, 128]), vv))
        if exch is not None:
            tasks.append((None, exch))
        tasks += qtasks
        self.run_tasks(tasks)

    def stage_mla_attn(self, l, L):
        P, I, Dm = self.P, self.I, self.Dm
        kva, kvc, qs = Dm["kva%d" % L], Dm["kvc%d" % L], Dm["qs%d" % L]
        Kb = self.h[:, 0:NKEY]
        Vb = self.h[:, NKEY:2 * NKEY].rearrange("p (c d) -> p c d", d=128)
        A4 = self.aux[:, :].rearrange("p (c t) -> p c t", c=8)
        Og = A4[:, 4:8, :]
        kva3 = kva.rearrange("(r n) t -> n r t", r=8)
        krb = self.wring[3]
        self.ring, self.wl_n = 3, 0
        scale = 192 ** -0.5
        hres = ["h%d.%d" % (c, t) for c in range(KC) for t in range(3)]
        P.dma("sp", krb[0:64, 0:4096].rearrange("p (r t) -> p r t", r=4), kva3[4096:4160, 0:4, :], slot="kr",
              reads=["dram_all_kv%d" % L], writes=["w3"])
        P.dma("sp", krb[0:64, 4096:4224], kva3[4096:4160, 4, 0:128], slot="kr", reads=["dram_all_kv%d" % L],
              writes=["w3"])
        P.dma("sp", krb[64:128, 0:896], kva3[4096:4160, 4, 128:1024], slot="kr", reads=["dram_all_kv%d" % L],
              writes=["w3"])
        P.dma("sp", krb[64:128, 896:3968].rearrange("p (r t) -> p r t", r=3), kva3[4096:4160, 5:8, :], slot="kr",
              reads=["dram_all_kv%d" % L], writes=["w3"])
        P.dma("sp", krb[64:128, 3968:4224], kvc[4096:4160, :], slot="kr", reads=["dram_kv"], writes=["w3"])
        for hd in range(16):
            g, hh = hd // 4, hd % 4
            self.kv_load(Kb, Vb, kva3, kvc, hd * 128, 2048 + hd * 128, L)
            qi = hd % 2
            Qn, Qr = A4[:, 2 * qi, :], A4[:, 2 * qi + 1, :]
            qres = "aq%d" % qi
            P.dma("sp", Qn, qs[hd * 128:(hd + 1) * 128, :], slot="q%d" % qi, reads=["dram_q"], writes=[qres, "aq"])
            P.dma("sp", Qr[0:64, :], qs[2048 + hd * 64:2048 + (hd + 1) * 64, :], slot="q%d" % qi, reads=["dram_q"],
                  writes=[qres])
            P.dma("sp", Qr[64:128, :], qs[2048 + hd * 64:2048 + (hd + 1) * 64, :], slot="q%d" % qi,
                  reads=["dram_q"], writes=[qres])
            for t in range(3):
                t0, tw = TILES[t]
                chunks = list(range(NCH)) if t != 2 else [64, 65]

                def kq(ck, t0=t0, tw=tw, Qn=Qn, Qr=Qr):
                    if ck < 33:
                        kr = (krb[0:64, ck * 128:(ck + 1) * 128], Qr[0:64, t0:t0 + tw])
                    else:
                        kr = (krb[64:128, (ck - 33) * 128:(ck - 32) * 128], Qr[64:128, t0:t0 + tw])
                    return [(Kb[:, ck * 128:(ck + 1) * 128], Qn[:, t0:t0 + tw]), kr]
                self.attn_unit(t0, tw, chunks, kq, lambda ck: Vb[:, ck, :], scale, Og[:, hh, t0:t0 + tw],
                               ["w3", qres], "ao")
            if hh == 3:
                self.run_tasks(self.wo_tasks(l, I["c_w_o"], g, [0, 1, 2], Og))
        self.ring, self.wl_n = NSLOT, 0

    def seg0(self):
        self.layer_vecs(0)
        self.stage_norm(0, 0, [0, 1, 2])
        ex = (lambda: self.exchange(self.Dm["kvo0"], self.Dm["kva0"], "kv0")) if 1 in self.segs else None
        self.stage_gqa_proj(0, 0, 0, True, ex)

    def seg1(self):
        if 0 not in self.segs:
            self.layer_vecs(0)
        if "noattn" not in DBG:
            self.stage_gqa_attn(0, 0, 0, True)
        if "nomlp" not in DBG:
            self.stage_mlp(0, [0, 1, 2])
        self.layer_vecs(1)
        if "noconv" not in DBG:
            self.stage_norm(1, 0, [0, 1, 2])
            self.stage_conv_main(1)
        if 2 in self.segs:
            self.exchange(self.Dm["halo_o"], self.Dm["halo_a"], "halo")

    def seg2(self):
        if 1 not in self.segs:
            self.layer_vecs(1)
        self.stage_conv_fix(1)
        self.stage_mlp(1, [0, 1, 2])
        self.layer_vecs(2)
        self.stage_norm(2, 0, [0, 1, 2])
        ex = (lambda: self.exchange(self.Dm["kvo2"], self.Dm["kva2"], "kv2")) if 3 in self.segs else None
        self.stage_mla_proj(2, 2, ex)

    def seg3(self):
        if 2 not in self.segs:
            self.layer_vecs(2)
        self.stage_mla_attn(2, 2)
        self.stage_mlp(2, [0, 1, 2])
        self.layer_vecs(3)
        self.stage_norm(3, 0, [0, 1, 2])
        ex = (lambda: self.exchange(self.Dm["kvo3"], self.Dm["kva3"], "kv3")) if 4 in self.segs else None
        self.stage_gqa_proj(3, 1, 3, False, ex)

    def seg4(self):
        if 3 not in self.segs:
            self.layer_vecs(3)
        self.stage_gqa_attn(3, 1, 3, False)
        self.stage_mlp(3, [0, 1])
        fn = self.small[:, 136:152]
        self.P.dma("sp", fn, self.I["fnT"], slot="c2", reads=["small"], writes=["fn"])
        self.I_fn = fn
        outT = self.dout("outT", [128, KC, NL])
        self.stage_norm(3, 0, [0, 1], final=True, outT=outT)


def _fm(v):
    v = np.asarray(v, dtype=np.float32)
    lead = v.shape[:-1]
    n = v.shape[-1] // 128
    return np.ascontiguousarray(np.moveaxis(v.reshape(*lead, n, 128), -1, 0))


def _consts():
    cm = np.zeros((128, 3, 128), np.float32)
    cm[:, 0, :] = 1.0
    k = np.arange(128)
    cm[k, 1, (k + 64) % 128] = 1.0
    cm[k, 2, k ^ 32] = 1.0
    return cm


def _rope_tables(r):
    tok = r * NL + np.arange(NL)
    row, col = (tok // 64).astype(np.float32), (tok % 64).astype(np.float32)
    k = np.arange(128)
    out = np.zeros((2, 128, 2, NL), np.float32)
    for bi, blk in enumerate((128, 64)):
        pp = k % blk
        half, quarter = blk // 2, blk // 4
        jj = pp % half
        inv = (np.float32(10000.0) ** (-(jj % quarter).astype(np.float32) / np.float32(quarter))).astype(np.float32)
        pos = np.where((jj < quarter)[:, None], row[None, :], col[None, :]).astype(np.float32)
        ang = (pos * inv[:, None]).astype(np.float32)
        sgn = np.where(pp < half, -1.0, 1.0).astype(np.float32)
        out[bi, :, 0, :] = np.cos(ang)
        out[bi, :, 1, :] = np.sin(ang) * sgn[:, None]
    return out


def _prepare(x, c, ctx, c_ctx, w_ada, b_ada, norm1, norm2, w_mlp1, w_mlp2,
             a_w_qkv, a_q_norm, a_k_norm, a_w_o, b_w_in, b_conv, b_w_out,
             c_w_dq, c_q_norm, c_w_uq, c_w_dkv, c_kv_norm, c_w_ukv, c_w_o, final_norm):
    f = lambda a: np.ascontiguousarray(np.asarray(a, dtype=np.float32))
    cm = _consts()
    x2, ctx2 = f(x)[0], f(ctx)[0]
    shared = {
        "cmat": cm,
        "n1T": np.ascontiguousarray(_fm(norm1)), "n2T": np.ascontiguousarray(_fm(norm2)),
        "w_mlp1": f(w_mlp1), "w_mlp2": f(w_mlp2),
        "cvec": np.ascontiguousarray(np.stack([_fm(f(c)[0]), _fm(c_ctx)], axis=-1)),
        "w_ada": f(w_ada), "b_adaT": np.ascontiguousarray(_fm(b_ada)),
        "a_w_qkv": f(a_w_qkv), "a_w_o": f(a_w_o),
        "a_qk": np.ascontiguousarray(np.stack([f(a_q_norm).T, f(a_k_norm).T], axis=-1)),
        "b_w_in": f(b_w_in)[0], "b_convT": np.ascontiguousarray(_fm(f(b_conv)[0])), "b_w_out": f(b_w_out)[0],
        "c_w_dq": f(c_w_dq)[0], "c_w_uq": f(c_w_uq)[0], "c_w_dkv": f(c_w_dkv)[0], "c_w_ukv": f(c_w_ukv)[0],
        "c_w_o": f(c_w_o)[0],
        "c_nT": np.ascontiguousarray(np.stack([_fm(f(c_q_norm)[0]), _fm(f(c_kv_norm)[0])], axis=1)),
        "fnT": np.ascontiguousarray(_fm(final_norm)),
    }
    per = []
    for r in range(NCORES):
        d = {}
        d["xT"] = np.ascontiguousarray(np.concatenate([x2[r * NL:(r + 1) * NL].T, ctx2.T], axis=1))
        d["ropetab"] = _rope_tables(r)
        hm = np.zeros((128, 2, 8), np.float32)
        if r > 0:
            hm[:, 0, r - 1] = 1.0
        if r < 7:
            hm[:, 1, r + 1] = 1.0
        d["hmask"] = hm
        d["w_ada_sh"] = (f(w_ada), r)
        d["b_adaT_sh"] = np.ascontiguousarray(shared["b_adaT"][:, :, r * 12:(r + 1) * 12])
        per.append(d)
    return shared, per


def _launch(segs, shared, per, state, cores=None):
    cores = list(range(NCORES)) if cores is None else cores
    b = Builder(segs)
    nc = b.build()
    in_maps = []
    for r in cores:
        m = {}
        for name in b.ext_in:
            if name in ("w_mlp1", "w_mlp2"):
                m[name] = np.ascontiguousarray(shared[name][b.mlp_layers])
            elif name == "w_ada_sh":
                wa, rr = per[r][name]
                m[name] = np.ascontiguousarray(wa[:, :, rr * 1536:(rr + 1) * 1536])
            elif name in per[r]:
                m[name] = per[r][name]
            elif name in shared:
                m[name] = shared[name]
            else:
                m[name] = state[r][name]
        in_maps.append(m)
    for name in ("w_mlp1", "w_mlp2"):
        if name in b.ext_in:
            for m in in_maps[1:]:
                m[name] = in_maps[0][name]
    res = run_bass_kernel_spmd(nc, in_maps, core_ids=list(range(len(cores))))
    outs = dict(zip(cores, res.results))
    for r in cores:
        o = outs[r]
        for name in b.ext_out:
            if name == "x_out":
                state[r]["x_in"] = o[name]
            elif name == "mods_out":
                state[r]["mods_in"] = o[name]
            else:
                state[r][name] = o[name]
    if len(cores) < NCORES:
        return state
    for name in b.ext_out:
        if name.startswith("kvo"):
            allv = np.concatenate([outs[r][name] for r in range(NCORES)], axis=0)
            for r in range(NCORES):
                state[r]["kva" + name[3:]] = allv
        if name == "halo_o":
            hs = [np.asarray(outs[r][name]).reshape(128, 16, 2) for r in range(NCORES)]
            for r in range(NCORES):
                lr = np.zeros((128, 16, 2), np.float32)
                if r > 0:
                    lr[:, :, 0] = hs[r - 1][:, :, 1]
                if r < 7:
                    lr[:, :, 1] = hs[r + 1][:, :, 0]
                state[r]["halo_lr"] = lr
    return state


def kernel(**inputs):
    shared, per = _prepare(**inputs)
    launches = [[0, 1, 2, 3, 4]] if FUSED else [[0], [1], [2], [3], [4]]
    state = [dict() for _ in range(NCORES)]
    for segs in launches:
        state = _launch(segs, shared, per, state)
    out = np.empty((1, NCORES * NL, D), np.float32)
    for r in range(NCORES):
        oT = np.asarray(state[r]["outT"])
        out[0, r * NL:(r + 1) * NL, :] = oT.transpose(2, 1, 0).reshape(NL, D)
    return out
```
